# Optimizing a Trainium2 kernel written in Bass

```python
import jax, jax.numpy as jnp
from jax import lax
import numpy as np

D_MODEL = 1024
BATCH = 32
SEQ = 2048
DEPTH = 1
DEC_BATCH = 8
DEC_SEQ = 32
PAST_LEN = 1024

CHUNK = 64
N_META = 16
MIX_WIDTH = D_MODEL
HG_WIDTH = MIX_WIDTH // 2
HG_HEADS = 4
HG_DIM = HG_WIDTH // HG_HEADS
ATT_WIDTH = MIX_WIDTH - HG_WIDTH
ATT_HEAD_DIM = 64
N_Q_HEADS = ATT_WIDTH // ATT_HEAD_DIM
N_KV_HEADS = 2
Q_PER_KV = N_Q_HEADS // N_KV_HEADS
KV_WIDTH = N_KV_HEADS * ATT_HEAD_DIM
WINDOW = 128
WIN_CHUNKS = WINDOW // CHUNK
ROPE_THETA = 10000.0
D_FF = -(-8 * D_MODEL // (3 * 256)) * 256
IN_COLS = 4 * HG_WIDTH + ATT_WIDTH + 2 * KV_WIDTH
EPS = 1e-6

kernel_name = 'hymba_hgrn2_swa_sink_streaming_step'


def rmsnorm(x, g):
    xf = x.astype(jnp.float32)
    y = xf * lax.rsqrt(jnp.mean(xf * xf, axis=-1, keepdims=True) + EPS)
    return y.astype(x.dtype) * g.astype(x.dtype)


def rope(x, pos):
    inv = ROPE_THETA ** (-jnp.arange(0, ATT_HEAD_DIM, 2, dtype=jnp.float32) / ATT_HEAD_DIM)
    ang = pos.astype(jnp.float32)[:, None] * inv[None, :]
    ang = jnp.concatenate([ang, ang], axis=-1)[:, None, :]
    xf = x.astype(jnp.float32)
    x1, x2 = jnp.split(xf, 2, axis=-1)
    rot = jnp.concatenate([-x2, x1], axis=-1)
    return (xf * jnp.cos(ang) + rot * jnp.sin(ang)).astype(x.dtype)


def project(hn, w, lb, pos):
    lead = hn.shape[:-1]
    idx = [HG_WIDTH, 2 * HG_WIDTH, 3 * HG_WIDTH, 4 * HG_WIDTH,
           4 * HG_WIDTH + ATT_WIDTH, 4 * HG_WIDTH + ATT_WIDTH + KV_WIDTH]
    hq, hf, hi, hg, aq, ak, av = jnp.split(hn @ w, idx, axis=-1)
    heads = lambda t: t.reshape(*lead, HG_HEADS, HG_DIM)
    fgate = lb + (1.0 - lb) * jax.nn.sigmoid(hf.astype(jnp.float32))
    q_h = heads(jax.nn.silu(hq.astype(jnp.float32)))
    logf = heads(jnp.log(fgate))
    k_h = heads(1.0 - fgate)
    v_h = heads(hi)
    g_h = heads(hg)
    aq = rope(aq.reshape(*lead, N_Q_HEADS, ATT_HEAD_DIM), pos).reshape(*lead, N_KV_HEADS, Q_PER_KV, ATT_HEAD_DIM)
    ak = rope(ak.reshape(*lead, N_KV_HEADS, ATT_HEAD_DIM), pos)
    av = av.reshape(*lead, N_KV_HEADS, ATT_HEAD_DIM)
    return q_h, logf, k_h, v_h, g_h, aq, ak, av


def hgrn_block(q, logf, k, v, S0):
    q, k, v = (t.astype(jnp.float32) for t in (q, k, v))
    L = q.shape[1]
    b = jnp.cumsum(logf.astype(jnp.float32), axis=1)
    o_inter = jnp.einsum('blhk,bhkv->blhv', q * jnp.exp(b), S0)
    causal = jnp.tril(jnp.ones((L, L), dtype=bool))
    diff = jnp.minimum(b[:, :, None] - b[:, None, :], 0.0)
    decay = jnp.where(causal[None, :, :, None, None], jnp.exp(diff), 0.0)
    A = jnp.einsum('bthk,btshk,bshk->bhts', q, decay, k)
    o_intra = jnp.einsum('bhts,bshv->bthv', A, v)
    bL = b[:, -1]
    S_new = jnp.exp(bL)[..., None] * S0 + jnp.einsum('bshk,bshv->bhkv', k * jnp.exp(bL[:, None] - b), v)
    return o_inter + o_intra, S_new


def hgrn_readout(o, g, gain):
    on = o * lax.rsqrt(jnp.mean(o * o, axis=-1, keepdims=True) + EPS)
    y = on * gain.astype(jnp.float32).reshape(HG_HEADS, HG_DIM) * jax.nn.silu(g.astype(jnp.float32))
    return y.reshape(*o.shape[:-2], HG_WIDTH)


def sink_attention(q, k, v, sinks, valid):
    s = jnp.einsum('...qgrd,...kgd->...grqk', q, k).astype(jnp.float32) * (ATT_HEAD_DIM ** -0.5)
    if valid is not None:
        s = jnp.where(valid[..., None, None, None, :], s, jnp.finfo(jnp.float32).min)
    sink = jnp.broadcast_to(sinks.astype(jnp.float32)[:, :, None, None], s.shape[:-1] + (1,))
    p = jax.nn.softmax(jnp.concatenate([s, sink], axis=-1), axis=-1)[..., :-1]
    return jnp.einsum('...grqk,...kgd->...qgrd', p.astype(v.dtype), v)


def finish_layer(h, o_hg, g_hg, o_att, hg_norm_l, attn_norm_l, w_out_l, norm2_l, w_ffn_in_l, w_ffn_out_l):
    mixed = jnp.concatenate([hgrn_readout(o_hg, g_hg, hg_norm_l).astype(h.dtype),
                             rmsnorm(o_att, attn_norm_l)], axis=-1) @ w_out_l
    h = h + mixed
    gate, up = jnp.split(rmsnorm(h, norm2_l) @ w_ffn_in_l, 2, axis=-1)
    return h + (jax.nn.silu(gate) * up) @ w_ffn_out_l


def band(t, nc):
    B = t.shape[0]
    pad = jnp.zeros((B, WIN_CHUNKS * CHUNK) + t.shape[2:], t.dtype)
    tp = jnp.concatenate([pad, t], axis=1).reshape(B, nc + WIN_CHUNKS, CHUNK, *t.shape[2:])
    return jnp.concatenate([tp[:, j:j + nc] for j in range(WIN_CHUNKS + 1)], axis=2)


def setup_inputs(seed: int = 0) -> dict:
    key = jax.random.key(seed)
    ks = jax.random.split(key, 24)
    n = lambda i, shape, s=1.0: jax.random.normal(ks[i], shape, jnp.float32) * s
    win_keep = min(WINDOW, PAST_LEN)
    kv_shape = lambda L: (DEPTH, DEC_BATCH, L, N_KV_HEADS, ATT_HEAD_DIM)
    return {
        'x_prompt': n(0, (BATCH, SEQ, D_MODEL)),
        'x_sample': n(1, (DEC_BATCH, DEC_SEQ, D_MODEL)),
        'cache_meta_k': n(2, kv_shape(N_META)),
        'cache_meta_v': n(3, kv_shape(N_META)),
        'cache_win_k': n(4, kv_shape(win_keep)),
        'cache_win_v': n(5, kv_shape(win_keep)),
        'state_hgrn': n(6, (DEPTH, DEC_BATCH, HG_HEADS, HG_DIM, HG_DIM), 0.3),
        'meta_tokens': n(7, (N_META, D_MODEL)),
        'norm1': 1.0 + n(8, (DEPTH, D_MODEL), 0.02),
        'w_in': n(9, (DEPTH, D_MODEL, IN_COLS), D_MODEL ** -0.5),
        'lb_param': n(10, (DEPTH + 1, HG_WIDTH), 0.1),
        'hg_norm': 1.0 + n(11, (DEPTH, HG_WIDTH), 0.02),
        'attn_sinks': n(12, (DEPTH, N_KV_HEADS, Q_PER_KV), 0.5),
        'attn_norm': 1.0 + n(13, (DEPTH, ATT_WIDTH), 0.02),
        'w_out': n(14, (DEPTH, MIX_WIDTH, D_MODEL), MIX_WIDTH ** -0.5),
        'norm2': 1.0 + n(15, (DEPTH, D_MODEL), 0.02),
        'w_ffn_in': n(16, (DEPTH, D_MODEL, 2 * D_FF), D_MODEL ** -0.5),
        'w_ffn_out': n(17, (DEPTH, D_FF, D_MODEL), D_FF ** -0.5),
        'final_norm': 1.0 + n(18, (D_MODEL,), 0.02),
    }


def reference(x_prompt, x_sample, cache_meta_k, cache_meta_v, cache_win_k, cache_win_v, state_hgrn,
              meta_tokens, norm1, w_in, lb_param, hg_norm, attn_sinks, attn_norm, w_out, norm2,
              w_ffn_in, w_ffn_out, final_norm):
    B, S, _ = x_prompt.shape
    Bd, T, _ = x_sample.shape
    nc = S // CHUNK
    keep = min(WINDOW, S)
    lb_all = jnp.cumsum(jax.nn.softmax(lb_param.astype(jnp.float32), axis=0), axis=0)
    pos_meta = jnp.arange(N_META)
    pos_p = N_META + jnp.arange(S)
    pos_s = N_META + PAST_LEN + jnp.arange(T)
    band_valid = (jnp.arange(nc)[:, None] + jnp.arange((WIN_CHUNKS + 1) * CHUNK)[None, :] // CHUNK - WIN_CHUNKS) >= 0
    valid_p = jnp.concatenate([jnp.ones((nc, N_META), dtype=bool), band_valid], axis=-1)

    h = x_prompt
    h_meta = meta_tokens[None].astype(x_prompt.dtype)
    hs = x_sample
    p_mk, p_mv, p_wk, p_wv, p_st, s_nk, s_nv, s_st = ([] for _ in range(8))
    for l in range(DEPTH):
        lb = lb_all[l]
        hm = rmsnorm(h_meta, norm1[l])
        mq, mlogf, mk, mv, mg, maq, mak, mav = project(hm, w_in[l], lb, pos_meta)
        mo_hg, S_meta = hgrn_block(mq, mlogf, mk, mv, jnp.zeros((1, HG_HEADS, HG_DIM, HG_DIM), jnp.float32))
        hn = rmsnorm(h, norm1[l])
        q, logf, k, v, g, aq, ak, av = project(hn, w_in[l], lb, pos_p)
        to_blocks = lambda t: t.reshape(B, nc, CHUNK, *t.shape[2:]).swapaxes(0, 1)

        def step(Sc, blk):
            o, Sn = hgrn_block(*blk, Sc)
            return Sn, o

        S_fin, o_blocks = lax.scan(step, jnp.broadcast_to(S_meta, (B, HG_HEADS, HG_DIM, HG_DIM)),
                                   (to_blocks(q), to_blocks(logf), to_blocks(k), to_blocks(v)))
        o_hg = o_blocks.swapaxes(0, 1).reshape(B, S, HG_HEADS, HG_DIM)
        qb = aq.reshape(B, nc, CHUNK, N_KV_HEADS, Q_PER_KV, ATT_HEAD_DIM)
        meta_kb = jnp.broadcast_to(mak[:, None], (B, nc, N_META, N_KV_HEADS, ATT_HEAD_DIM))
        meta_vb = jnp.broadcast_to(mav[:, None], (B, nc, N_META, N_KV_HEADS, ATT_HEAD_DIM))
        keys = jnp.concatenate([meta_kb, band(ak, nc)], axis=2)
        vals = jnp.concatenate([meta_vb, band(av, nc)], axis=2)
        o_att = sink_attention(qb, keys, vals, attn_sinks[l], valid_p).reshape(B, S, ATT_WIDTH)
        h = finish_layer(h, o_hg, g, o_att, hg_norm[l], attn_norm[l], w_out[l], norm2[l], w_ffn_in[l], w_ffn_out[l])
        p_mk.append(jnp.broadcast_to(mak, (B, N_META, N_KV_HEADS, ATT_HEAD_DIM)))
        p_mv.append(jnp.broadcast_to(mav, (B, N_META, N_KV_HEADS, ATT_HEAD_DIM)))
        p_wk.append(ak[:, S - keep:])
        p_wv.append(av[:, S - keep:])
        p_st.append(S_fin.astype(x_prompt.dtype))
        if l < DEPTH - 1:
            mo_att = sink_attention(maq, mak, mav, attn_sinks[l], None).reshape(1, N_META, ATT_WIDTH)
            h_meta = finish_layer(h_meta, mo_hg, mg, mo_att, hg_norm[l], attn_norm[l], w_out[l], norm2[l], w_ffn_in[l], w_ffn_out[l])
        hn_s = rmsnorm(hs, norm1[l])
        sq, slogf, sk, sv, sg, saq, sak, sav = project(hn_s, w_in[l], lb, pos_s)
        so_hg, S_s = hgrn_block(sq, slogf, sk, sv, state_hgrn[l].astype(jnp.float32))
        skeys = jnp.concatenate([cache_meta_k[l], cache_win_k[l], sak], axis=1)
        svals = jnp.concatenate([cache_meta_v[l], cache_win_v[l], sav], axis=1)
        so_att = sink_attention(saq, skeys, svals, attn_sinks[l], None).reshape(Bd, T, ATT_WIDTH)
        hs = finish_layer(hs, so_hg, sg, so_att, hg_norm[l], attn_norm[l], w_out[l], norm2[l], w_ffn_in[l], w_ffn_out[l])
        s_nk.append(sak)
        s_nv.append(sav)
        s_st.append(S_s.astype(state_hgrn.dtype))

    y_prompt = rmsnorm(h, final_norm)
    y_sample = rmsnorm(hs, final_norm)
    return (y_prompt, y_sample, jnp.stack(p_mk), jnp.stack(p_mv), jnp.stack(p_wk), jnp.stack(p_wv),
            jnp.stack(p_st), jnp.stack(s_nk), jnp.stack(s_nv), jnp.stack(s_st))
```

```python
import numpy as np
import concourse.bass as bass
import concourse.mybir as mybir
from concourse.bass_utils import run_bass_kernel_spmd
from concourse.alu_op_type import AluOpType as ALU

F32 = mybir.dt.float32
BF16 = mybir.dt.bfloat16
AF = mybir.ActivationFunctionType
AX = mybir.AxisListType

N_CORES = 8
D = 1024
KC = 8
SEQ = 2048
NCOL = 2816
DFF = 2816
NFC = 22
EPS = 1e-6
NW1 = 2
NW2 = 2
PAST_LEN = 1024
N_META = 16
DEC_SEQ = 32
DBG_TILE = 0
SAFE_ENGINES = set()
ORDER_OVERRIDE = None


class Buf:
    __slots__ = ("name", "w", "r", "small", "excl", "unread")

    def __init__(self, name, small=False, excl=False):
        self.name = name
        self.w = None
        self.r = {}
        self.small = small
        self.excl = excl
        self.unread = False


class Tok:
    __slots__ = ("sem", "val", "src", "key")

    def __init__(self, sem, val, src, key):
        self.sem = sem
        self.val = val
        self.src = src
        self.key = key


class Sched:
    def __init__(self, nc, safe_same=False):
        self.nc = nc
        self.safe_same = safe_same
        self.eng = {}
        for name, h in (("pe", nc.tensor), ("act", nc.scalar), ("dve", nc.vector),
                        ("pool", nc.gpsimd), ("sp", nc.sync)):
            self.eng[name] = {"h": h, "sem": nc.alloc_semaphore("sem_" + name), "cnt": 0, "waited": {}}
        self.dsem = {}
        self.nwait = 0

    def _deps(self, r, w):
        deps = []
        for b in r:
            if b.excl:
                deps.append((b.w, b))
                deps.extend((t, b) for t in b.r.values())
            elif b.w is not None:
                deps.append((b.w, b))
        for b in w:
            deps.append((b.w, b))
            deps.extend((t, b) for t in b.r.values())
        return deps

    def _waits(self, en, deps):
        e = self.eng[en]
        for tok, b in deps:
            if tok is None:
                continue
            if tok.src == en:
                if en == "pe":
                    continue
                if not (b.small or self.safe_same or en in SAFE_ENGINES):
                    continue
            if e["waited"].get(tok.key, 0) < tok.val:
                e["h"].wait_ge(tok.sem, tok.val)
                e["waited"][tok.key] = tok.val
                self.nwait += 1

    def _commit(self, tok, en, r, w):
        for b in w:
            b.w = tok
            b.r = {}
            if b.excl:
                b.unread = True
        for b in r:
            if b.excl:
                b.w = tok
                b.r = {}
                b.unread = False
            else:
                b.r[en] = tok

    def op(self, en, fn, r=(), w=(), sig=True):
        e = self.eng[en]
        self._waits(en, self._deps(r, w))
        inst = fn()
        if sig:
            e["cnt"] += 1
            inst.then_inc(e["sem"], 1)
            tok = Tok(e["sem"], e["cnt"], en, en)
        else:
            tok = Tok(e["sem"], e["cnt"] + 1, en, en)
        self._commit(tok, en, r, w)
        return inst

    def dma(self, q, out, in_, r=(), w=(), key=None):
        e = self.eng[q]
        self._waits(q, self._deps(r, w))
        if key not in self.dsem:
            self.dsem[key] = [self.nc.alloc_semaphore("dsem_" + key), 0]
        ds = self.dsem[key]
        if ds[1] > 0 and e["waited"].get("dma:" + key, 0) < ds[1]:
            e["h"].wait_ge(ds[0], ds[1])
            e["waited"]["dma:" + key] = ds[1]
        inst = e["h"].dma_start(out=out, in_=in_)
        ds[1] += 16
        inst.then_inc(ds[0], 16)
        tok = Tok(ds[0], ds[1], "dma:" + key, "dma:" + key)
        self._commit(tok, "dma:" + key, r, w)
        return inst

    def finish(self, q="sp"):
        e = self.eng[q]
        for key, (sem, cnt) in self.dsem.items():
            if cnt > 0:
                e["h"].wait_ge(sem, cnt)


def build(ns=4, nt=16, do_sample=True, safe_same=True, dbg=False):
    assert nt % 4 == 0
    nc = bass.Bass("TRN2", target_bir_lowering=False, dynamic_dma_scratch_size=4096)
    C = Sched(nc, safe_same=safe_same)
    V, A, P, T, S = nc.vector, nc.scalar, nc.gpsimd, nc.tensor, nc.sync

    def din(name, shape, dt=F32):
        return nc.dram_tensor(name, list(shape), dt, kind="ExternalInput").ap()

    def dout(name, shape, dt=F32):
        return nc.dram_tensor(name, list(shape), dt, kind="ExternalOutput").ap()

    xp = din("xp", [ns * nt * 128, D])
    xs = din("xs", [DEC_SEQ, D])
    xm = din("xm", [N_META, D])
    w_in_d = din("w_in", [128, KC * NCOL])
    w_out_d = din("w_out", [128, KC * D])
    w1_d = din("w1", [NFC, 128, KC * 256])
    w2_d = din("w2", [4, 128, NFC * 256])
    gT_d = din("gT", [128, 24])
    gf_d = din("gf", [1, D])
    lb_d = din("lbT", [128, 8])
    sink_d = din("sinks", [1, 8])
    cos_d = din("cosT", [128, 18 * 32])
    sin_d = din("sinT", [128, 18 * 32])
    cmk_d = din("cmk", [N_META, 128])
    cmv_d = din("cmv", [N_META, 128])
    cwk_d = din("cwk", [128, 128])
    cwv_d = din("cwv", [128, 128])
    st_d = din("st0", [4, 128, 128])

    yp = dout("yp", [ns * nt * 128, D])
    ys = dout("ys", [DEC_SEQ, D])
    pmk = dout("pmk", [ns, N_META, 128])
    pmv = dout("pmv", [ns, N_META, 128])
    pwk = dout("pwk", [ns, 128, 128])
    pwv = dout("pwv", [ns, 128, 128])
    pst = dout("pst", [ns, 4, 128, 128])
    snk = dout("snk", [DEC_SEQ, 128])
    snv = dout("snv", [DEC_SEQ, 128])
    sst = dout("sst", [4, 128, 128])

    w1s = nc.dram_tensor("w1s", [NFC, 128, KC * 256], BF16, kind="Internal").ap()
    w2s = nc.dram_tensor("w2s", [4, 128, NFC * 256], BF16, kind="Internal").ap()
    b_w1s = [Buf("w1s%d" % i) for i in range(NFC)]
    b_w2s = [Buf("w2s%d" % i) for i in range(4)]

    def sb(name, shape, dt=F32, small=False):
        t = nc.alloc_sbuf_tensor("s_" + name, list(shape), dt).ap()
        return t, Buf(name, small=small)

    def sbn(name, shape, n, dt=F32):
        t = nc.alloc_sbuf_tensor("s_" + name, [shape[0], n] + list(shape[1:]), dt).ap()
        return t, [Buf("%s%d" % (name, i)) for i in range(n)]

    w_in, b_w_in = sb("w_in_sb", [128, KC, NCOL], BF16)
    w_out, b_w_out = sb("w_out_sb", [128, KC, D], BF16)
    w1r, b_w1r = sbn("w1r", [128, KC, 256], NW1, BF16)
    w2r, b_w2r = sbn("w2r", [128, NFC, 256], NW2, BF16)
    h, b_h = sbn("h", [128, D], 8)
    xst, b_xst = sb("xst", [128, D])
    xn, b_xn = sbn("xn", [128, D], 2, BF16)
    xnT, b_xnT = sbn("xnT", [128, KC, 128], 2, BF16)
    hn2T, b_hn2T = sbn("hn2T", [128, KC, 512], 2, BF16)
    actT, b_actT = sb("actT", [128, NFC, 512], BF16)
    sgt, b_sgt = sbn("sgt", [128, 512], 2, BF16)
    tn, b_tn = sb("tn", [128, 512])
    lf, b_lf = sb("lf", [128, 512])
    bb, b_bb = sb("bb", [128, 512])
    enb, b_enb = sb("enb", [128, 512], BF16)
    sq, b_sq = sb("sq", [128, 512], BF16)
    sgg, b_sgg = sbn("sgg", [128, 512], 2, BF16)
    osq, b_osq = sb("osq", [128, 512])
    qT, b_qT = sbn("qT", [128, 512], 2, BF16)
    kfT, b_kfT = sbn("kfT", [128, 512], 2, BF16)
    ktok, b_ktok = sb("ktok", [128, 512], BF16)
    vb, b_vb = sbn("vb", [128, 512], 2, BF16)
    ATb, b_ATb = sb("ATb", [128, 512], BF16)
    Sst, b_S = sb("S", [128, 512])
    Stmp, b_Stmp = lf, b_lf
    Sbf, b_Sbf = sb("Sbf", [128, 512], BF16)
    qf, b_qf = sb("qf", [128, 512])
    oatt, b_oatt = qf, b_qf
    kvf, b_kvf = sb("kvf", [128, 256])
    rt, b_rt = sb("rt", [128, 512])
    Qr, b_Qr = sbn("Qr", [128, 512], 2, BF16)
    QTz, b_QTz = sbn("QTz", [128, 512], 2, BF16)
    Kr, b_Kr = sbn("Kr", [128, 128], 2, BF16)
    kTc, b_kTc = sbn("kTc", [128, 128], 2, BF16)
    kTm, b_kTm = sb("kTm", [128, 128], BF16)
    kTms, b_kTms = sb("kTms", [128, 16], BF16)
    vext, b_vext = sbn("vext", [128, 2, 66], 3, BF16)
    vextm, b_vextm = sb("vextm", [128, 2, 66], BF16)
    vextms, b_vextms = sb("vextms", [16, 2, 66], BF16)
    PTp, b_PTp = sbn("PTp", [128, 512], 2, BF16)
    PTc, b_PTc = sbn("PTc", [128, 512], 2, BF16)
    PTm, b_PTm = sbn("PTm", [128, 512], 2, BF16)
    ldtmp, b_ldtmp = rt[:, 0:256], b_rt
    ebL, b_ebL = sbn("ebL", [128, 4], 2)
    ident, b_ident = sb("ident", [128, 128], BF16)
    maskT, b_maskT = sb("maskT", [128, 128], BF16)
    ones1, b_ones = sb("ones", [128, 2], BF16)
    ones = ones1[:, 0:1].to_broadcast([128, 128])
    gT, b_gT = sb("gT", [128, 24])
    gfb, b_gfb = sb("gfb", [128, D], BF16)
    lbt, b_lbt = sb("lbt", [128, 8])
    cvec, b_cvec = sb("cvec", [128, 4], small=True)
    esink, b_esink = sb("esink", [128, 8], small=True)
    cosT, b_cos = sb("cosT", [128, 18, 32], BF16)
    sinT, b_sin = sb("sinT", [128, 18, 32], BF16)
    mhalf, b_mhalf = sb("mhalf", [128, 4], small=True)
    stat, _ = sb("stat", [128, 48])
    _stat_i = [0]

    def small(name, n):
        i = _stat_i[0]
        _stat_i[0] += n
        assert _stat_i[0] <= 48
        return stat[:, i:i + n], Buf(name, small=True)

    ss1, b_ss1 = small("ss1", 1)
    ms1, b_ms1 = small("ms1", 1)
    rs1, b_rs1 = small("rs1", 1)
    ssh, b_ssh = small("ssh", 4)
    msh, b_msh = small("msh", 4)
    rsh, b_rsh = small("rsh", 4)
    ssa, b_ssa = small("ssa", 1)
    msa, b_msa = small("msa", 1)
    rsa, b_rsa = small("rsa", 1)
    ss2, b_ss2 = small("ss2", 1)
    ms2, b_ms2 = small("ms2", 1)
    rs2, b_rs2 = small("rs2", 1)
    fin_stat = [small("fin%d" % i, 4) for i in range(4)]
    den, b_den = small("den", 4)
    rec, b_rec = small("rec", 4)

    smeta_d = nc.dram_tensor("smeta_d", [128, 512], F32, kind="Internal").ap()
    b_smeta_d = Buf("smeta_d")

    banks = []
    for i in range(8):
        t = nc.alloc_psum_tensor("bank%d" % i, [128, 512], F32).ap()
        banks.append((t, Buf("bank%d" % i, excl=True)))
    mix_rr = [0]

    def nb():
        i = mix_rr[0] % 5
        mix_rr[0] += 1
        assert not banks[i][1].unread, "PSUM bank %d reused before its data was read" % i
        return banks[i]

    ffn_rr = [0]

    def nfb():
        i = 5 + ffn_rr[0] % 3
        ffn_rr[0] += 1
        assert not banks[i][1].unread, "PSUM bank %d reused before its data was read" % i
        return banks[i]

    g1T = gT[:, 0:8]
    g2T = gT[:, 8:16]
    gmT = gT[:, 16:24]

    C.dma("sp", gT, gT_d, w=[b_gT], key="c_gT")
    C.dma("pool", gfb, gf_d.broadcast_to([128, D]), w=[b_gfb], key="c_gf")
    C.dma("sp", lbt, lb_d, w=[b_lbt], key="c_lb")
    C.dma("sp", esink, sink_d.broadcast_to([128, 8]), w=[b_esink], key="c_sink")
    C.dma("pool", cosT, cos_d.rearrange("p (a b) -> p a b", a=18), w=[b_cos], key="c_cos")
    C.dma("pool", sinT, sin_d.rearrange("p (a b) -> p a b", a=18), w=[b_sin], key="c_sin")
    C.dma("pool", w_in, w_in_d.rearrange("p (k n) -> p k n", k=KC), w=[b_w_in], key="w_in")
    C.dma("pool", w_out, w_out_d.rearrange("p (k n) -> p k n", k=KC), w=[b_w_out], key="w_out")
    for i in range(NFC):
        C.dma("pool", w1s[i], w1_d[i], r=[b_w_in, b_w_out], w=[b_w1s[i]], key="w1s%d" % i)
    for i in range(4):
        C.dma("pool", w2s[i], w2_d[i], r=[b_w_in, b_w_out], w=[b_w2s[i]], key="w2s%d" % i)

    C.op("pool", lambda: P.memset(ident, 1.0), w=[b_ident])
    C.op("pool", lambda: P.affine_select(out=ident, in_=ident, pattern=[[-1, 128]], compare_op=ALU.is_equal,
                                         fill=0.0, base=0, channel_multiplier=1), r=[b_ident], w=[b_ident])
    C.op("pool", lambda: P.memset(maskT, 1.0), w=[b_maskT])
    C.op("pool", lambda: P.affine_select(out=maskT, in_=maskT, pattern=[[1, 128]], compare_op=ALU.is_ge,
                                         fill=0.0, base=0, channel_multiplier=-1), r=[b_maskT], w=[b_maskT])
    C.op("pool", lambda: P.memset(ones1, 1.0), w=[b_ones])
    C.op("pool", lambda: P.memset(mhalf, -0.5), w=[b_mhalf])
    C.op("pool", lambda: P.memset(Sst, 0.0), w=[b_S])
    C.op("pool", lambda: P.memset(Sbf, 0.0), w=[b_Sbf])
    for i in range(3):
        C.op("pool", lambda i=i: P.memset(vext[:, i], 1.0), w=[b_vext[i]])
    for i in range(2):
        C.op("pool", lambda i=i: P.memset(PTp[:, i], 0.0), w=[b_PTp[i]])
        C.op("pool", lambda i=i: P.memset(PTc[:, i], 0.0), w=[b_PTc[i]])
    C.op("pool", lambda: P.memset(QTz, 0.0), w=[b_QTz[0], b_QTz[1]])
    C.op("pool", lambda: P.memset(vextm, 1.0), w=[b_vextm])
    C.op("pool", lambda: P.memset(kTm, 0.0), w=[b_kTm])
    C.op("pool", lambda: P.memset(PTm, 0.0), w=[b_PTm[0], b_PTm[1]])
    C.op("pool", lambda: P.memset(vextms, 1.0), w=[b_vextms])
    C.op("dve", lambda: V.tensor_tensor(out=cvec, in0=lbt[:, 0:4], in1=lbt[:, 4:8], op=ALU.subtract),
         r=[b_lbt], w=[b_cvec])
    C.op("act", lambda: A.activation(out=cvec, in_=cvec, func=AF.Exp), r=[b_cvec], w=[b_cvec])
    C.op("dve", lambda: V.tensor_scalar(out=cvec, in0=cvec, scalar1=1.0, scalar2=2.0, op0=ALU.add, op1=ALU.mult),
         r=[b_cvec], w=[b_cvec])
    C.op("dve", lambda: V.reciprocal(out=cvec, in_=cvec), r=[b_cvec], w=[b_cvec])
    C.op("act", lambda: A.activation(out=esink, in_=esink, func=AF.Exp), r=[b_esink], w=[b_esink])

    n_f1_passes = ns * (nt // 4) + (1 if do_sample else 0)
    n_groups_total = ns * (nt // 4) + (1 if do_sample else 0)

    def issue_w1(n):
        if n >= n_f1_passes * NFC:
            return
        slot = n % NW1
        C.dma("sp", w1r[:, slot], w1s[n % NFC].rearrange("p (k n) -> p k n", k=KC),
              r=[b_w1s[n % NFC]], w=[b_w1r[slot]], key="w1r%d" % slot)

    def issue_w2(m):
        if m >= n_groups_total * 4:
            return
        slot = m % NW2
        C.dma("sp", w2r[:, slot], w2s[m % 4].rearrange("p (k n) -> p k n", k=NFC),
              r=[b_w2s[m % 4]], w=[b_w2r[slot]], key="w2r%d" % slot)

    for n in range(NW1):
        issue_w1(n)
    for m in range(NW2):
        issue_w2(m)

    def rstd_chain(ssx, b_ssx, msx, b_msx, rsx, b_rsx, n, inv):
        C.op("pool", lambda: P.tensor_scalar(out=msx, in0=ssx, scalar1=inv, scalar2=EPS, op0=ALU.mult, op1=ALU.add),
             r=[b_ssx], w=[b_msx])
        C.op("pool", lambda: P.tensor_tensor(out=rsx, in0=msx, in1=mhalf[:, 0:n], op=ALU.pow),
             r=[b_msx, b_mhalf], w=[b_rsx])

    def transposes8(src, b_src, dst3, b_dst, gain):
        bk, b_bk = nb()
        bkb = bk.bitcast(BF16)
        for kc in range(KC):
            C.op("pe", lambda kc=kc: T.transpose(out=bkb[:, kc * 128:(kc + 1) * 128],
                                                 in_=src[:, kc * 128:(kc + 1) * 128], identity=ident),
                 r=[b_src, b_ident], w=[b_bk], sig=(kc == KC - 1))
        C.op("dve", lambda: V.tensor_tensor(out=dst3, in0=bkb.rearrange("p (k t) -> p k t", k=KC),
                                            in1=gain.unsqueeze(2).to_broadcast([128, KC, 128]), op=ALU.mult),
             r=[b_bk, b_gT], w=[b_dst])

    def rope(src4, sin_b, cos_b4, tmp4, dst4, r_bufs, b_src, b_tmp, b_dst):
        C.op("pool", lambda: P.tensor_tensor(out=tmp4[:, :, 0, :], in0=src4[:, :, 1, :], in1=sin_b, op=ALU.mult),
             r=[b_src] + r_bufs, w=[b_tmp])
        C.op("pool", lambda: P.tensor_tensor(out=tmp4[:, :, 1, :], in0=src4[:, :, 0, :], in1=sin_b, op=ALU.mult),
             r=[b_src] + r_bufs, w=[b_tmp])
        C.op("pool", lambda: P.tensor_tensor(out=src4, in0=src4, in1=cos_b4, op=ALU.mult),
             r=[b_src, b_tmp] + r_bufs, w=[b_src])
        C.op("pool", lambda: P.tensor_tensor(out=dst4[:, :, 0, :], in0=src4[:, :, 0, :], in1=tmp4[:, :, 0, :],
                                             op=ALU.subtract), r=[b_src, b_tmp], w=[b_dst])
        C.op("pool", lambda: P.tensor_tensor(out=dst4[:, :, 1, :], in0=src4[:, :, 1, :], in1=tmp4[:, :, 1, :],
                                             op=ALU.add), r=[b_src, b_tmp], w=[b_dst])

    def load_x(t):
        nv = t["nvalid"]
        if nv < 128:
            C.op("pool", lambda: P.memset(xst, 0.0), w=[b_xst])
        C.dma("act", xst[0:nv], t["src"], w=[b_xst], key="xst")

    def load_h(t):
        if t["kind"] == "meta":
            return
        hs = t["hslot"]
        nv = t["nvalid"]
        if nv < 128:
            C.op("pool", lambda: P.memset(h[:, hs], 0.0), w=[b_h[hs]])
        C.dma("sp", h[0:nv, hs], t["src"], w=[b_h[hs]], key="xh%d" % hs)

    def front(t):
        p = t["p"]
        meta = t["kind"] == "meta"
        L = t["nvalid"]
        ridx = t["ridx"]
        x_t, bx = xst, b_xst
        xn_t, bxn = xn[:, p], b_xn[p]
        xnT_t, bxnT = xnT[:, p], b_xnT[p]

        def norm1():
            C.op("act", lambda: A.activation(out=xn_t, in_=x_t, func=AF.Square, accum_out=ss1), r=[bx],
                 w=[bxn, b_ss1])
            rstd_chain(ss1, b_ss1, ms1, b_ms1, rs1, b_rs1, 1, 1.0 / D)
            C.op("dve", lambda: V.tensor_scalar(out=xn_t, in0=x_t, scalar1=rs1, scalar2=None, op0=ALU.mult),
                 r=[bx, b_rs1], w=[bxn])

        def xnT_():
            transposes8(xn_t, bxn, xnT_t, bxnT, g1T)

        def proj_a(bk, b_bk, c0):
            for hh in range(4):
                for kc in range(KC):
                    C.op("pe", lambda hh=hh, kc=kc: T.matmul(
                        bk[:, hh * 128:(hh + 1) * 128], lhsT=w_in[:, kc, c0 + hh * 128:c0 + (hh + 1) * 128],
                        rhs=xnT_t[:, kc, :], start=(kc == 0), stop=(kc == KC - 1)),
                        r=[b_w_in, bxnT], w=[b_bk], sig=(hh == 3 and kc == KC - 1))

        def proj_b(bk, b_bk, c0, n):
            for kc in range(KC):
                C.op("pe", lambda kc=kc: T.matmul(
                    bk[:, 0:n], lhsT=xnT_t[:, kc, :], rhs=w_in[:, kc, c0:c0 + n],
                    start=(kc == 0), stop=(kc == KC - 1)),
                    r=[b_w_in, bxnT], w=[b_bk], sig=(kc == KC - 1))

        def qf_():
            bkF, b_bkF = nb()
            proj_a(bkF, b_bkF, 512)
            C.op("act", lambda: A.activation(out=tn, in_=bkF, func=AF.Tanh, scale=-0.5), r=[b_bkF], w=[b_tn])
            if not meta:
                bkQ, b_bkQ = nb()
                proj_a(bkQ, b_bkQ, 0)
                C.op("act", lambda: A.activation(out=sq, in_=bkQ, func=AF.Silu), r=[b_bkQ], w=[b_sq])

        def vg_():
            bkV, b_bkV = nb()
            proj_b(bkV, b_bkV, 1024, 512)
            C.op("act", lambda: A.activation(out=vb[:, p], in_=bkV, func=AF.Copy), r=[b_bkV], w=[b_vb[p]])
            if not meta:
                bkG, b_bkG = nb()
                proj_b(bkG, b_bkG, 1536, 512)
                C.op("act", lambda: A.activation(out=sgg[:, p], in_=bkG, func=AF.Silu), r=[b_bkG], w=[b_sgg[p]])

        def kvq_():
            bkKV, b_bkKV = nb()
            proj_b(bkKV, b_bkKV, 2560, 256)
            C.op("act", lambda: A.activation(out=kvf, in_=bkKV[:, 0:256], func=AF.Copy), r=[b_bkKV], w=[b_kvf])
            if not meta:
                bkAQ, b_bkAQ = nb()
                proj_b(bkAQ, b_bkAQ, 2048, 512)
                C.op("act", lambda: A.activation(out=qf, in_=bkAQ, func=AF.Copy), r=[b_bkAQ], w=[b_qf])
            sin_k = sinT[:, ridx, :].unsqueeze(1).to_broadcast([128, 2, 32])
            cos_k4 = cosT[:, ridx, :].unsqueeze(1).unsqueeze(1).to_broadcast([128, 2, 2, 32])
            k4 = kvf[:, 0:128].rearrange("p (a b c) -> p a b c", a=2, b=2)
            rt4k = rt[:, 0:128].rearrange("p (a b c) -> p a b c", a=2, b=2)
            rope(k4, sin_k, cos_k4, rt4k, k4, [b_sin, b_cos], b_kvf, b_rt, b_kvf)
            C.op("pool", lambda: P.tensor_copy(out=Kr[:, p], in_=kvf[:, 0:128]), r=[b_kvf], w=[b_Kr[p]])
            outs = t["outs"]
            if "k" in outs:
                for i, (dst, nrow) in enumerate(outs["k"]):
                    C.dma("sp", dst, kvf[0:nrow, 0:128], r=[b_kvf], key="o_k%d" % i)
                for i, (dst, nrow) in enumerate(outs["v"]):
                    C.dma("sp", dst, kvf[0:nrow, 128:256], r=[b_kvf], key="o_v%d" % i)
            if meta:
                C.op("act", lambda: A.activation(out=vextm[0:16, :, 0:64],
                                                 in_=kvf[0:16, 128:256].rearrange("p (a d) -> p a d", a=2),
                                                 func=AF.Copy), r=[b_kvf], w=[b_vextm])
            else:
                C.op("act", lambda: A.activation(out=vext[:, t["v3"], :, 0:64],
                                                 in_=kvf[:, 128:256].rearrange("p (a d) -> p a d", a=2),
                                                 func=AF.Copy), r=[b_kvf], w=[b_vext[t["v3"]]])
                sin_q = sinT[:, ridx, :].unsqueeze(1).to_broadcast([128, 8, 32])
                cos_q4 = cosT[:, ridx, :].unsqueeze(1).unsqueeze(1).to_broadcast([128, 8, 2, 32])
                q4 = qf.rearrange("p (a b c) -> p a b c", a=8, b=2)
                rt4 = rt.rearrange("p (a b c) -> p a b c", a=8, b=2)
                Qr4 = Qr[:, p].rearrange("p (a b c) -> p a b c", a=8, b=2)
                rope(q4, sin_q, cos_q4, rt4, Qr4, [b_sin, b_cos], b_qf, b_rt, b_Qr[p])

        def gates1():
            for hh in range(4):
                C.op("act", lambda hh=hh: A.activation(out=tn[:, hh * 128:(hh + 1) * 128],
                                                       in_=tn[:, hh * 128:(hh + 1) * 128], func=AF.Identity,
                                                       scale=cvec[:, hh:hh + 1], bias=cvec[:, hh:hh + 1]),
                     r=[b_tn, b_cvec], w=[b_tn])
            C.op("act", lambda: A.activation(out=lf, in_=tn, func=AF.Ln, scale=-1.0, bias=1.0), r=[b_tn], w=[b_lf])

        def gates2():
            for hh in range(4):
                C.op("dve", lambda hh=hh: V.tensor_tensor_scan(
                    out=bb[:, hh * 128:(hh + 1) * 128], data0=ones, data1=lf[:, hh * 128:(hh + 1) * 128],
                    initial=0.0, op0=ALU.mult, op1=ALU.add), r=[b_lf, b_ones], w=[b_bb])

        def gates3():
            C.op("act", lambda: A.activation(out=enb, in_=bb, func=AF.Exp, scale=-1.0), r=[b_bb], w=[b_enb])
            C.op("act", lambda: A.activation(out=bb, in_=bb, func=AF.Exp), r=[b_bb], w=[b_bb])

        def gates4():
            if not meta:
                C.op("dve", lambda: V.tensor_tensor(out=qT[:, p], in0=sq, in1=bb, op=ALU.mult),
                     r=[b_sq, b_bb], w=[b_qT[p]])
            C.op("dve", lambda: V.tensor_tensor(out=kfT[:, p], in0=tn, in1=enb, op=ALU.mult),
                 r=[b_tn, b_enb], w=[b_kfT[p]])
            C.op("dve", lambda: V.tensor_copy(out=ebL[:, p], in_=bb.rearrange("p (a t) -> p a t", a=4)[:, :, L - 1]),
                 r=[b_bb], w=[b_ebL[p]])

        return {"norm1": norm1, "xnT": xnT_, "qf": qf_, "vg": vg_, "kvq": kvq_, "gates1": gates1,
                "gates2": gates2, "gates3": gates3, "gates4": gates4}

    def back(t):
        p = t["p"]
        pp = 1 - p
        kind = t["kind"]
        meta = kind == "meta"
        sample = kind == "sample"
        L = t["nvalid"]
        hslot = t["hslot"]
        has_prev = t["has_prev"]
        outs = t["outs"]
        kTm_use, b_kTm_use = (kTms, b_kTms) if sample else (kTm, b_kTm)
        vxm_use, b_vxm_use = (vextms, b_vextms) if sample else (vextm, b_vextm)
        ncur = L if sample else 128
        xn_t, bxn = xn[:, p], b_xn[p]
        mixedT, b_mixedT = xnT[:, p], b_xnT[p]
        mixed, b_mixed = xn_t, bxn
        hsl = (t["grp"] or 0) % 2
        vc = t["v3"]
        vp = (vc + 2) % 3

        def init():
            if t.get("init_state") == "meta":
                C.dma("sp", Sst, smeta_d, r=[b_smeta_d], w=[b_S], key="ld_sm")
                C.op("act", lambda: A.activation(out=Sbf, in_=Sst, func=AF.Copy), r=[b_S], w=[b_Sbf])
            elif t.get("init_state") == "cache":
                C.dma("sp", Sst.rearrange("p (a v) -> p a v", a=4), st_d.rearrange("a k v -> k a v"), w=[b_S],
                      key="ld_s")
                C.op("act", lambda: A.activation(out=Sbf, in_=Sst, func=AF.Copy), r=[b_S], w=[b_Sbf])

        def qk_tr():
            bkT, b_bkT = nb()
            bkTb = bkT.bitcast(BF16)
            if not meta:
                for t4 in range(4):
                    C.op("pe", lambda t4=t4: T.transpose(out=bkTb[:, t4 * 128:(t4 + 1) * 128],
                                                         in_=Qr[:, p, t4 * 128:(t4 + 1) * 128], identity=ident),
                         r=[b_Qr[p], b_ident], w=[b_bkT], sig=False)
            C.op("pe", lambda: T.transpose(out=bkTb[:, 512:640], in_=Kr[:, p], identity=ident),
                 r=[b_Kr[p], b_ident], w=[b_bkT])
            if meta:
                C.op("dve", lambda: V.tensor_copy(out=kTm[:, 0:16], in_=bkTb[:, 512:528]), r=[b_bkT], w=[b_kTm])
            else:
                C.op("dve", lambda: V.tensor_copy(out=QTz[0:64, 0], in_=bkTb[0:64, 0:512]), r=[b_bkT], w=[b_QTz[0]])
                C.op("dve", lambda: V.tensor_copy(out=QTz[64:128, 1], in_=bkTb[64:128, 0:512]), r=[b_bkT],
                     w=[b_QTz[1]])
                C.op("dve", lambda: V.tensor_copy(out=kTc[:, p], in_=bkTb[:, 512:640]), r=[b_bkT], w=[b_kTc[p]])

        def hgrn_a():
            if not meta:
                bkA, b_bkA = nb()
                for hh in range(4):
                    C.op("pe", lambda hh=hh: T.matmul(bkA[:, hh * 128:(hh + 1) * 128],
                                                      lhsT=kfT[:, p, hh * 128:(hh + 1) * 128],
                                                      rhs=qT[:, p, hh * 128:(hh + 1) * 128], start=True, stop=True),
                         r=[b_kfT[p], b_qT[p]], w=[b_bkA], sig=(hh == 3))
                C.op("dve", lambda: V.tensor_tensor(out=ATb.rearrange("p (a t) -> p a t", a=4),
                                                    in0=bkA.rearrange("p (a t) -> p a t", a=4),
                                                    in1=maskT.unsqueeze(1).to_broadcast([128, 4, 128]),
                                                    op=ALU.mult), r=[b_bkA, b_maskT], w=[b_ATb])
            bkTk, b_bkTk = nb()
            bkTkb = bkTk.bitcast(BF16)
            for hh in range(4):
                C.op("pe", lambda hh=hh: T.transpose(out=bkTkb[:, hh * 128:(hh + 1) * 128],
                                                     in_=kfT[:, p, hh * 128:(hh + 1) * 128], identity=ident),
                     r=[b_kfT[p], b_ident], w=[b_bkTk], sig=(hh == 3))
            C.op("act", lambda: A.activation(out=ktok, in_=bkTkb[:, 0:512], func=AF.Copy), r=[b_bkTk], w=[b_ktok])

        def scores(g):
            if meta:
                return
            if has_prev:
                bkX, b_bkX = nb()
                C.op("pe", lambda: T.matmul(bkX, lhsT=kTc[:, pp, :], rhs=QTz[:, g], start=True, stop=True),
                     r=[b_kTc[pp], b_QTz[g]], w=[b_bkX])
                if sample:
                    C.op("act", lambda: A.activation(out=PTp[:, g], in_=bkX, func=AF.Exp, scale=0.125),
                         r=[b_bkX], w=[b_PTp[g]])
                else:
                    C.op("act", lambda: A.activation(out=PTp[64:128, g], in_=bkX[64:128, :], func=AF.Exp,
                                                     scale=0.125), r=[b_bkX], w=[b_PTp[g]])
                    C.op("act", lambda: A.activation(
                        out=PTp[0:64, g].rearrange("p (a t) -> p a t", a=4)[:, :, 0:64],
                        in_=bkX[0:64, :].rearrange("p (a t) -> p a t", a=4)[:, :, 0:64], func=AF.Exp, scale=0.125),
                        r=[b_bkX], w=[b_PTp[g]])
            bkY, b_bkY = nb()
            C.op("pe", lambda: T.matmul(bkY[0:ncur, :], lhsT=kTc[:, p, 0:ncur], rhs=QTz[:, g], start=True,
                                        stop=True), r=[b_kTc[p], b_QTz[g]], w=[b_bkY])
            if sample:
                C.op("act", lambda: A.activation(out=PTc[0:ncur, g], in_=bkY[0:ncur, :], func=AF.Exp, scale=0.125),
                     r=[b_bkY], w=[b_PTc[g]])
            else:
                C.op("act", lambda: A.activation(out=PTc[0:64, g], in_=bkY[0:64, :], func=AF.Exp, scale=0.125),
                     r=[b_bkY], w=[b_PTc[g]])
                C.op("act", lambda: A.activation(
                    out=PTc[64:128, g].rearrange("p (a t) -> p a t", a=4)[:, :, 64:128],
                    in_=bkY[64:128, :].rearrange("p (a t) -> p a t", a=4)[:, :, 64:128], func=AF.Exp, scale=0.125),
                    r=[b_bkY], w=[b_PTc[g]])
            bkM, b_bkM = nb()
            if sample:
                C.op("pe", lambda: T.matmul(bkM[0:16, :], lhsT=kTm_use, rhs=QTz[:, g], start=True, stop=True),
                     r=[b_kTm_use, b_QTz[g]], w=[b_bkM])
            else:
                C.op("pe", lambda: T.matmul(bkM, lhsT=kTm_use, rhs=QTz[:, g], start=True, stop=True),
                     r=[b_kTm_use, b_QTz[g]], w=[b_bkM])
            C.op("act", lambda: A.activation(out=PTm[0:16, g], in_=bkM[0:16, :], func=AF.Exp, scale=0.125),
                 r=[b_bkM], w=[b_PTm[g]])

        def pv(g):
            if meta:
                return
            bkP, b_bkP = nb()
            for r4 in range(4):
                terms = []
                if has_prev:
                    terms.append((PTp[:, g, r4 * 128:(r4 + 1) * 128], vext[:, vp, g, 0:65], [b_PTp[g], b_vext[vp]]))
                terms.append((PTc[0:ncur, g, r4 * 128:(r4 + 1) * 128], vext[0:ncur, vc, g, 0:65],
                              [b_PTc[g], b_vext[vc]]))
                if sample:
                    terms.append((PTm[0:16, g, r4 * 128:(r4 + 1) * 128], vxm_use[:, g, 0:65],
                                  [b_PTm[g], b_vxm_use]))
                else:
                    terms.append((PTm[:, g, r4 * 128:(r4 + 1) * 128], vxm_use[:, g, 0:65], [b_PTm[g], b_vxm_use]))
                for ti, (lt, rh, rb) in enumerate(terms):
                    C.op("pe", lambda lt=lt, rh=rh, ti=ti, r4=r4: T.matmul(
                        bkP[:, r4 * 65:(r4 + 1) * 65], lhsT=lt, rhs=rh, start=(ti == 0),
                        stop=(ti == len(terms) - 1)), r=rb, w=[b_bkP],
                        sig=(r4 == 3 and ti == len(terms) - 1))
            bkP3 = bkP[:, 0:260].rearrange("p (a d) -> p a d", a=4)
            C.op("dve", lambda: V.tensor_tensor(out=den, in0=bkP3[:, :, 64], in1=esink[:, g * 4:(g + 1) * 4],
                                                op=ALU.add), r=[b_bkP, b_esink], w=[b_den])
            C.op("dve", lambda: V.reciprocal(out=rec, in_=den), r=[b_den], w=[b_rec])
            C.op("dve", lambda: V.tensor_tensor(
                out=oatt[:, g * 256:(g + 1) * 256].rearrange("p (a d) -> p a d", a=4), in0=bkP3[:, :, 0:64],
                in1=rec.unsqueeze(2).to_broadcast([128, 4, 64]), op=ALU.mult), r=[b_bkP, b_rec], w=[b_oatt])
            if g == 1:
                C.op("act", lambda: A.activation(out=osq, in_=oatt, func=AF.Square, accum_out=ssa),
                     r=[b_oatt], w=[b_osq, b_ssa])
                rstd_chain(ssa, b_ssa, msa, b_msa, rsa, b_rsa, 1, 1.0 / 512)
                C.op("act", lambda: A.activation(out=mixed[:, 512:1024], in_=oatt, func=AF.Copy, scale=rsa),
                     r=[b_oatt, b_rsa], w=[b_mixed])

        def hgrn_o():
            if not meta:
                bkO, b_bkO = nb()
                for hh in range(4):
                    C.op("pe", lambda hh=hh: T.matmul(bkO[:, hh * 128:(hh + 1) * 128],
                                                      lhsT=qT[:, p, hh * 128:(hh + 1) * 128],
                                                      rhs=Sbf[:, hh * 128:(hh + 1) * 128], start=True, stop=False),
                         r=[b_qT[p], b_Sbf], w=[b_bkO], sig=False)
                    C.op("pe", lambda hh=hh: T.matmul(bkO[:, hh * 128:(hh + 1) * 128],
                                                      lhsT=ATb[:, hh * 128:(hh + 1) * 128],
                                                      rhs=vb[:, p, hh * 128:(hh + 1) * 128], start=False, stop=True),
                         r=[b_ATb, b_vb[p]], w=[b_bkO], sig=(hh == 3))
                C.op("act", lambda: A.activation(out=osq, in_=bkO, func=AF.Square), r=[b_bkO], w=[b_osq])
                C.op("dve", lambda: V.tensor_reduce(out=ssh, in_=osq.rearrange("p (a t) -> p a t", a=4),
                                                    axis=AX.X, op=ALU.add), r=[b_osq], w=[b_ssh])
                rstd_chain(ssh, b_ssh, msh, b_msh, rsh, b_rsh, 4, 1.0 / 128)
                C.op("dve", lambda: V.tensor_tensor(out=osq, in0=bkO, in1=sgg[:, p], op=ALU.mult),
                     r=[b_bkO, b_sgg[p]], w=[b_osq])
                C.op("dve", lambda: V.tensor_tensor(out=mixed[:, 0:512].rearrange("p (a t) -> p a t", a=4),
                                                    in0=osq.rearrange("p (a t) -> p a t", a=4),
                                                    in1=rsh.unsqueeze(2).to_broadcast([128, 4, 128]), op=ALU.mult),
                     r=[b_osq, b_rsh], w=[b_mixed])
            bkS, b_bkS = nb()
            for hh in range(4):
                C.op("pe", lambda hh=hh: T.matmul(bkS[:, hh * 128:(hh + 1) * 128],
                                                  lhsT=ktok[:, hh * 128:(hh + 1) * 128],
                                                  rhs=vb[:, p, hh * 128:(hh + 1) * 128], start=True, stop=True),
                     r=[b_ktok, b_vb[p]], w=[b_bkS], sig=(hh == 3))
            C.op("dve", lambda: V.tensor_tensor(out=Stmp, in0=bkS, in1=Sst, op=ALU.add), r=[b_bkS, b_S], w=[b_Stmp])
            C.op("dve", lambda: V.tensor_tensor(out=Sst.rearrange("p (a t) -> p a t", a=4),
                                                in0=Stmp.rearrange("p (a t) -> p a t", a=4),
                                                in1=ebL[:, p].unsqueeze(2).to_broadcast([128, 4, 128]), op=ALU.mult),
                 r=[b_Stmp, b_ebL[p]], w=[b_S])
            C.op("act", lambda: A.activation(out=Sbf, in_=Sst, func=AF.Copy), r=[b_S], w=[b_Sbf])
            if meta:
                C.dma("sp", smeta_d, Sst, r=[b_S], w=[b_smeta_d], key="st_sm")
            if "state" in outs:
                C.dma("sp", outs["state"].rearrange("a k v -> k a v"), Sst.rearrange("p (a v) -> p a v", a=4),
                      r=[b_S], key="o_state")

        def mixT():
            if meta:
                return
            transposes8(mixed, b_mixed, mixedT, b_mixedT, gmT)

        def oproj():
            if meta:
                return
            for cg in range(2):
                bkC, b_bkC = nb()
                for kc in range(KC):
                    C.op("pe", lambda kc=kc: T.matmul(bkC, lhsT=mixedT[:, kc, :],
                                                      rhs=w_out[:, kc, cg * 512:(cg + 1) * 512],
                                                      start=(kc == 0), stop=(kc == KC - 1)),
                         r=[b_mixedT, b_w_out], w=[b_bkC], sig=(kc == KC - 1))
                C.op("dve", lambda: V.tensor_tensor(out=h[:, hslot, cg * 512:(cg + 1) * 512], in0=bkC,
                                                    in1=h[:, hslot, cg * 512:(cg + 1) * 512], op=ALU.add),
                     r=[b_bkC, b_h[hslot]], w=[b_h[hslot]])
            C.op("act", lambda: A.activation(out=xn_t, in_=h[:, hslot], func=AF.Square, accum_out=ss2),
                 r=[b_h[hslot]], w=[bxn, b_ss2])
            rstd_chain(ss2, b_ss2, ms2, b_ms2, rs2, b_rs2, 1, 1.0 / D)
            C.op("dve", lambda: V.tensor_scalar(out=xn_t, in0=h[:, hslot], scalar1=rs2, scalar2=None, op0=ALU.mult),
                 r=[b_h[hslot], b_rs2], w=[bxn])

        def hn2T_():
            if meta:
                return
            col = t["hn2_col"]
            transposes8(xn_t, bxn, hn2T[:, hsl, :, col * 128:(col + 1) * 128], b_hn2T[hsl], g2T)

        return {"init": init, "qk_tr": qk_tr, "scores0": lambda: scores(0), "scores1": lambda: scores(1),
                "hgrn_a": hgrn_a, "pv0": lambda: pv(0), "pv1": lambda: pv(1), "hgrn_o": hgrn_o,
                "mixT": mixT, "oproj": oproj, "hn2T": hn2T_}

    w1_ctr = [0]
    w2_ctr = [0]

    def f1_steps(hsl, N):
        c0 = 0
        steps = []
        for fc in range(NFC):
            def step(fc=fc):
                n = w1_ctr[0]
                w1_ctr[0] += 1
                slot = n % NW1
                bkG, b_bkG = nfb()
                bkU, b_bkU = nfb()
                for (bk, b_bk, cc) in ((bkG, b_bkG, 0), (bkU, b_bkU, 128)):
                    for kc in range(KC):
                        C.op("pe", lambda bk=bk, kc=kc, cc=cc: T.matmul(
                            bk[:, 0:N], lhsT=w1r[:, slot, kc, cc:cc + 128], rhs=hn2T[:, hsl, kc, c0:c0 + N],
                            start=(kc == 0), stop=(kc == KC - 1)),
                            r=[b_w1r[slot], b_hn2T[hsl]], w=[b_bk], sig=(kc == KC - 1))
                issue_w1(n + NW1)
                ss = n % 2
                C.op("act", lambda: A.activation(out=sgt[:, ss, 0:N], in_=bkG[:, 0:N], func=AF.Silu),
                     r=[b_bkG], w=[b_sgt[ss]])
                C.op("dve", lambda: V.tensor_tensor(out=actT[:, fc, c0:c0 + N], in0=bkU[:, 0:N], in1=sgt[:, ss, 0:N],
                                                    op=ALU.mult), r=[b_bkU, b_sgt[ss]], w=[b_actT])
            steps.append(step)
        return steps

    def f2_steps(hslots):
        steps = []
        for q in range(4):
            for st, hs in enumerate(hslots):
                def step(q=q, st=st, hs=hs):
                    if st == 0:
                        w2_ctr[0] += 1
                    m = w2_ctr[0] - 1
                    slot = m % NW2
                    bkF, b_bkF = nfb()
                    for fc in range(NFC):
                        C.op("pe", lambda fc=fc: T.matmul(bkF[:, 0:256], lhsT=actT[:, fc, st * 128:(st + 1) * 128],
                                                          rhs=w2r[:, slot, fc, :], start=(fc == 0),
                                                          stop=(fc == NFC - 1)),
                             r=[b_actT, b_w2r[slot]], w=[b_bkF], sig=(fc == NFC - 1))
                    C.op("dve", lambda: V.tensor_tensor(out=h[:, hs, q * 256:(q + 1) * 256], in0=bkF[:, 0:256],
                                                        in1=h[:, hs, q * 256:(q + 1) * 256], op=ALU.add),
                         r=[b_bkF, b_h[hs]], w=[b_h[hs]])
                    if st == len(hslots) - 1:
                        issue_w2(m + NW2)
                steps.append(step)
        return steps

    def fin_steps(hslots, y_dsts):
        stepsA, stepsB = [], []
        for st, hs in enumerate(hslots):
            sA, b_sA = fin_stat[st % 4]

            def stepA(hs=hs, sA=sA, b_sA=b_sA):
                C.op("act", lambda: A.activation(out=osq, in_=h[:, hs, 0:512], func=AF.Square, accum_out=sA[:, 0:1]),
                     r=[b_h[hs]], w=[b_osq, b_sA])
                C.op("act", lambda: A.activation(out=osq, in_=h[:, hs, 512:1024], func=AF.Square,
                                                 accum_out=sA[:, 1:2]), r=[b_h[hs]], w=[b_osq, b_sA])
                C.op("pool", lambda: P.tensor_tensor(out=sA[:, 0:1], in0=sA[:, 0:1], in1=sA[:, 1:2], op=ALU.add),
                     r=[b_sA], w=[b_sA])
                C.op("pool", lambda: P.tensor_scalar(out=sA[:, 2:3], in0=sA[:, 0:1], scalar1=1.0 / D, scalar2=EPS,
                                                     op0=ALU.mult, op1=ALU.add), r=[b_sA], w=[b_sA])
                C.op("pool", lambda: P.tensor_tensor(out=sA[:, 3:4], in0=sA[:, 2:3], in1=mhalf[:, 0:1], op=ALU.pow),
                     r=[b_sA, b_mhalf], w=[b_sA])

            def stepB(st=st, hs=hs, sA=sA, b_sA=b_sA):
                C.op("dve", lambda: V.scalar_tensor_tensor(out=h[:, hs], in0=h[:, hs], scalar=sA[:, 3:4], in1=gfb,
                                                           op0=ALU.mult, op1=ALU.mult),
                     r=[b_h[hs], b_sA, b_gfb], w=[b_h[hs]])
                dst, nrow = y_dsts[st]
                C.dma("sp", dst, h[0:nrow, hs], r=[b_h[hs]], key="o_y%d" % hs)
            stepsA.append(stepA)
            stepsB.append(stepB)
        return stepsA + stepsB

    tiles = []
    tiles.append(dict(kind="meta", src=xm, nvalid=N_META, ridx=17, has_prev=False, hn2_col=0,
                      outs={"k": [(pmk[s], N_META) for s in range(ns)], "v": [(pmv[s], N_META) for s in range(ns)]},
                      grp=None))
    gidx = 0
    if do_sample:
        tiles.append(dict(kind="sample", src=xs, nvalid=DEC_SEQ, ridx=16, has_prev=True, hn2_col=0,
                          outs={"k": [(snk, DEC_SEQ)], "v": [(snv, DEC_SEQ)], "state": sst},
                          init_state="cache", grp=gidx, gpos=0, ydst=(ys, DEC_SEQ)))
        gidx += 1
    for s in range(ns):
        for j in range(nt):
            row0 = (s * nt + j) * 128
            outs = {}
            if j == nt - 1:
                outs = {"k": [(pwk[s], 128)], "v": [(pwv[s], 128)], "state": pst[s]}
            tl = dict(kind="prompt", src=xp[row0:row0 + 128, :], nvalid=128, ridx=j, has_prev=j > 0,
                      hn2_col=j % 4, outs=outs, grp=gidx, gpos=j % 4,
                      ydst=(yp[row0:row0 + 128, :], 128))
            if j == 0:
                tl["init_state"] = "meta"
            if j % 4 == 3:
                gidx += 1
            tiles.append(tl)
    for i, tl in enumerate(tiles):
        tl["p"] = i % 2
        tl["v3"] = i % 3
        tl["hslot"] = i % 8

    def sample_prep(p_prev, v_prev):
        C.dma("sp", ldtmp[:, 0:128], cwk_d, w=[b_ldtmp], key="ld_c0")
        C.dma("sp", ldtmp[:, 128:256], cwv_d, w=[b_ldtmp], key="ld_c1")
        C.op("pool", lambda: P.tensor_copy(out=Kr[:, p_prev], in_=ldtmp[:, 0:128]), r=[b_ldtmp], w=[b_Kr[p_prev]])
        bk, b_bk = nb()
        bkb = bk.bitcast(BF16)
        C.op("pe", lambda: T.transpose(out=bkb[:, 0:128], in_=Kr[:, p_prev], identity=ident),
             r=[b_Kr[p_prev], b_ident], w=[b_bk])
        C.op("dve", lambda: V.tensor_copy(out=kTc[:, p_prev], in_=bkb[:, 0:128]), r=[b_bk], w=[b_kTc[p_prev]])
        C.op("act", lambda: A.activation(out=vext[:, v_prev, :, 0:64],
                                         in_=ldtmp[:, 128:256].rearrange("p (a d) -> p a d", a=2), func=AF.Copy),
             r=[b_ldtmp], w=[b_vext[v_prev]])
        C.dma("sp", ldtmp[0:16, 0:128], cmk_d, w=[b_ldtmp], key="ld_c0")
        C.dma("sp", ldtmp[0:16, 128:256], cmv_d, w=[b_ldtmp], key="ld_c1")
        C.op("pool", lambda: P.tensor_copy(out=Kr[:, p_prev], in_=ldtmp[:, 0:128]), r=[b_ldtmp], w=[b_Kr[p_prev]])
        bk, b_bk = nb()
        bkb = bk.bitcast(BF16)
        C.op("pe", lambda: T.transpose(out=bkb[:, 0:128], in_=Kr[:, p_prev], identity=ident),
             r=[b_Kr[p_prev], b_ident], w=[b_bk])
        C.op("dve", lambda: V.tensor_copy(out=kTms, in_=bkb[:, 0:16]), r=[b_bk], w=[b_kTms])
        C.op("act", lambda: A.activation(out=vextms[:, :, 0:64],
                                         in_=ldtmp[0:16, 128:256].rearrange("p (a d) -> p a d", a=2), func=AF.Copy),
             r=[b_ldtmp], w=[b_vextms])

    ORDER = ["F.norm1", "B.init", "B.qk_tr", "FILL1", "F.xnT", "B.scores0", "B.scores1", "FILL1", "F.kvq",
             "B.hgrn_a", "FILL1", "B.pv0", "B.pv1", "FILL2", "B.hgrn_o", "FILL1", "F.qf", "F.gates1", "B.mixT",
             "FILL2", "F.gates2", "B.oproj", "FILL2", "F.gates3", "F.vg", "FILL1", "F.gates4", "B.hn2T"]

    if ORDER_OVERRIDE is not None:
        ORDER = list(ORDER_OVERRIDE)

    from collections import deque
    fillq = deque()

    def pop_fill(k):
        for _ in range(k):
            if not fillq:
                return
            fillq.popleft()[1]()

    def drain_through(grp, kinds):
        last = -1
        for idx, (tag, _) in enumerate(fillq):
            if tag[0] < grp or (tag[0] == grp and tag[1] in kinds):
                last = idx
        pop_fill(last + 1)

    def group_tiles(g):
        return [tl for tl in tiles if tl["grp"] == g]

    load_x(tiles[0])
    for i in range(len(tiles) + 1):
        bt = tiles[i - 1] if i >= 1 else None
        ft = tiles[i] if i < len(tiles) else None
        if bt is not None and bt["kind"] == "sample":
            sample_prep(1 - bt["p"], (bt["v3"] + 2) % 3)
        if bt is not None:
            if bt["grp"] is not None and bt["grp"] >= 2:
                drain_through(bt["grp"] - 2, ("A", "B", "F2", "fin"))
            load_h(bt)
        Fd = front(ft) if ft is not None else {}
        Bd = back(bt) if bt is not None else {}
        for name in ORDER:
            if name.startswith("FILL"):
                pop_fill(int(name[4:]))
                continue
            if name == "B.hn2T" and bt is not None and bt["grp"] is not None and bt["grp"] >= 2:
                drain_through(bt["grp"] - 2, ("A",))
            d = Fd if name[0] == "F" else Bd
            fn = d.get(name[2:])
            if fn is not None:
                fn()
            if name == "F.norm1" and i + 1 < len(tiles):
                load_x(tiles[i + 1])
        if bt is not None and bt["kind"] == "sample":
            for g2 in range(2):
                C.op("pool", lambda g2=g2: P.memset(PTp[0:64, g2], 0.0), w=[b_PTp[g2]])
                C.op("pool", lambda g2=g2: P.memset(PTc[64:128, g2], 0.0), w=[b_PTc[g2]])
        if bt is not None and bt["grp"] is not None:
            g = bt["grp"]
            gt = group_tiles(g)
            if bt["kind"] == "sample":
                for st in f1_steps(g % 2, 128):
                    fillq.append(((g, "A"), st))
                for st in f2_steps([bt["hslot"]]):
                    fillq.append(((g, "F2"), st))
                for st in fin_steps([bt["hslot"]], [bt["ydst"]]):
                    fillq.append(((g, "fin"), st))
            elif bt["gpos"] == 3:
                for st in f1_steps(g % 2, 512):
                    fillq.append(((g, "A"), st))
                hs = [tl["hslot"] for tl in gt]
                for st in f2_steps(hs):
                    fillq.append(((g, "F2"), st))
                for st in fin_steps(hs, [tl["ydst"] for tl in gt]):
                    fillq.append(((g, "fin"), st))
    pop_fill(len(fillq))
    C.finish("sp")
    return nc


def _rope_tables():
    inv = 10000.0 ** (-np.arange(0, 64, 2, dtype=np.float64) / 64.0)
    pos = np.zeros((128, 18), np.float64)
    for j in range(16):
        pos[:, j] = N_META + j * 128 + np.arange(128)
    pos[:, 16] = N_META + PAST_LEN + np.arange(128)
    pos[:, 17] = np.arange(128)
    ang = pos[:, :, None] * inv[None, None, :]
    return (np.cos(ang).astype(np.float32).reshape(128, 18 * 32),
            np.sin(ang).astype(np.float32).reshape(128, 18 * 32))


def _layout_shared(w_in, w_out, w_ffn_in, w_ffn_out, norm1, norm2, hg_norm, attn_norm, final_norm, lb_param,
                   attn_sinks, meta_tokens):
    perm = list(range(2048))
    for t in range(4):
        perm += list(range(2048 + t * 64, 2048 + (t + 1) * 64))
        perm += list(range(2048 + (4 + t) * 64, 2048 + (5 + t) * 64))
    perm += list(range(2560, 2816))
    wi = np.ascontiguousarray(w_in[0][:, perm].reshape(KC, 128, NCOL).transpose(1, 0, 2).reshape(128, KC * NCOL))
    wo = np.ascontiguousarray(w_out[0].reshape(KC, 128, D).transpose(1, 0, 2).reshape(128, KC * D))
    w1 = w_ffn_in[0].reshape(KC, 128, 2, NFC, 128)
    w1 = np.ascontiguousarray(w1.transpose(3, 1, 0, 2, 4).reshape(NFC, 128, KC * 256))
    w2 = w_ffn_out[0].reshape(NFC, 128, 4, 256)
    w2 = np.ascontiguousarray(w2.transpose(2, 1, 0, 3).reshape(4, 128, NFC * 256))
    gm = np.concatenate([hg_norm[0], attn_norm[0]])
    gT = np.ascontiguousarray(np.concatenate([norm1[0].reshape(KC, 128).T, norm2[0].reshape(KC, 128).T,
                                              gm.reshape(KC, 128).T], axis=1))
    lbT = np.ascontiguousarray(lb_param.reshape(2, 4, 128).transpose(2, 0, 1).reshape(128, 8))
    cosT, sinT = _rope_tables()
    return {"w_in": wi, "w_out": wo, "w1": w1, "w2": w2, "gT": gT, "gf": np.ascontiguousarray(final_norm.reshape(1, D)),
            "lbT": lbT, "sinks": np.ascontiguousarray(attn_sinks.reshape(1, 8)), "cosT": cosT, "sinT": sinT,
            "xm": np.ascontiguousarray(meta_tokens)}


_NC_CACHE = {}


def kernel(x_prompt, x_sample, cache_meta_k, cache_meta_v, cache_win_k, cache_win_v, state_hgrn,
           meta_tokens, norm1, w_in, lb_param, hg_norm, attn_sinks, attn_norm, w_out, norm2,
           w_ffn_in, w_ffn_out, final_norm):
    f = lambda a: np.asarray(a, dtype=np.float32)
    (x_prompt, x_sample, cache_meta_k, cache_meta_v, cache_win_k, cache_win_v, state_hgrn, meta_tokens, norm1,
     w_in, lb_param, hg_norm, attn_sinks, attn_norm, w_out, norm2, w_ffn_in, w_ffn_out, final_norm) = map(
        f, (x_prompt, x_sample, cache_meta_k, cache_meta_v, cache_win_k, cache_win_v, state_hgrn, meta_tokens,
            norm1, w_in, lb_param, hg_norm, attn_sinks, attn_norm, w_out, norm2, w_ffn_in, w_ffn_out, final_norm))
    ns, nt = 4, 16
    if "nc" not in _NC_CACHE:
        _NC_CACHE["nc"] = build(ns, nt, True)
    nc = _NC_CACHE["nc"]
    shared = _layout_shared(w_in, w_out, w_ffn_in, w_ffn_out, norm1, norm2, hg_norm, attn_norm, final_norm,
                            lb_param, attn_sinks, meta_tokens)
    in_maps = []
    for c in range(N_CORES):
        m = dict(shared)
        m["xp"] = np.ascontiguousarray(x_prompt[c * ns:(c + 1) * ns].reshape(ns * SEQ, D))
        m["xs"] = np.ascontiguousarray(x_sample[c])
        m["cmk"] = np.ascontiguousarray(cache_meta_k[0, c].reshape(N_META, 128))
        m["cmv"] = np.ascontiguousarray(cache_meta_v[0, c].reshape(N_META, 128))
        m["cwk"] = np.ascontiguousarray(cache_win_k[0, c].reshape(128, 128))
        m["cwv"] = np.ascontiguousarray(cache_win_v[0, c].reshape(128, 128))
        m["st0"] = np.ascontiguousarray(state_hgrn[0, c])
        in_maps.append(m)
    res = run_bass_kernel_spmd(nc, in_maps, core_ids=list(range(N_CORES)))
    R = res.results
    cat = lambda k: np.concatenate([np.asarray(r[k]) for r in R], axis=0)
    y_prompt = cat("yp").reshape(32, SEQ, D)
    y_sample = np.stack([np.asarray(r["ys"]) for r in R], axis=0)
    p_meta_k = cat("pmk").reshape(1, 32, N_META, 2, 64)
    p_meta_v = cat("pmv").reshape(1, 32, N_META, 2, 64)
    p_win_k = cat("pwk").reshape(1, 32, 128, 2, 64)
    p_win_v = cat("pwv").reshape(1, 32, 128, 2, 64)
    p_state = cat("pst").reshape(1, 32, 4, 128, 128)
    s_new_k = np.stack([np.asarray(r["snk"]) for r in R], axis=0).reshape(1, 8, DEC_SEQ, 2, 64)
    s_new_v = np.stack([np.asarray(r["snv"]) for r in R], axis=0).reshape(1, 8, DEC_SEQ, 2, 64)
    s_state = np.stack([np.asarray(r["sst"]) for r in R], axis=0).reshape(1, 8, 4, 128, 128)
    return tuple(np.ascontiguousarray(a, dtype=np.float32) for a in (
        y_prompt, y_sample, p_meta_k, p_meta_v, p_win_k, p_win_v, p_state, s_new_k, s_new_v, s_state))
```

```python
import numpy as np
import concourse.bass as bass
import concourse.mybir as mybir
from concourse.bass_utils import run_bass_kernel_spmd
from concourse.alu_op_type import AluOpType as ALU

F32 = mybir.dt.float32
BF16 = mybir.dt.bfloat16
AF = mybir.ActivationFunctionType
AX = mybir.AxisListType

N_CORES = 8
D = 1024
KC = 8
SEQ = 2048
NCOL = 2816
DFF = 2816
NFC = 22
EPS = 1e-6
NW1 = 2
NW2 = 2
PAST_LEN = 1024
N_META = 16
DEC_SEQ = 32
DBG_TILE = 0
SAFE_ENGINES = set()
ORDER_OVERRIDE = None


class Buf:
    __slots__ = ("name", "w", "r", "small", "excl", "unread")

    def __init__(self, name, small=False, excl=False):
        self.name = name
        self.w = None
        self.r = {}
        self.small = small
        self.excl = excl
        self.unread = False


class Tok:
    __slots__ = ("sem", "val", "src", "key")

    def __init__(self, sem, val, src, key):
        self.sem = sem
        self.val = val
        self.src = src
        self.key = key


class Sched:
    def __init__(self, nc, safe_same=False):
        self.nc = nc
        self.safe_same = safe_same
        self.eng = {}
        for name, h in (("pe", nc.tensor), ("act", nc.scalar), ("dve", nc.vector),
                        ("pool", nc.gpsimd), ("sp", nc.sync)):
            self.eng[name] = {"h": h, "sem": nc.alloc_semaphore("sem_" + name), "cnt": 0, "waited": {}}
        self.dsem = {}
        self.nwait = 0

    def _deps(self, r, w):
        deps = []
        for b in r:
            if b.excl:
                deps.append((b.w, b))
                deps.extend((t, b) for t in b.r.values())
            elif b.w is not None:
                deps.append((b.w, b))
        for b in w:
            deps.append((b.w, b))
            deps.extend((t, b) for t in b.r.values())
        return deps

    def _waits(self, en, deps):
        e = self.eng[en]
        for tok, b in deps:
            if tok is None:
                continue
            if tok.src == en:
                if en == "pe":
                    continue
                if not (b.small or self.safe_same or en in SAFE_ENGINES):
                    continue
            if e["waited"].get(tok.key, 0) < tok.val:
                e["h"].wait_ge(tok.sem, tok.val)
                e["waited"][tok.key] = tok.val
                self.nwait += 1

    def _commit(self, tok, en, r, w):
        for b in w:
            b.w = tok
            b.r = {}
            if b.excl:
                b.unread = True
        for b in r:
            if b.excl:
                b.w = tok
                b.r = {}
                b.unread = False
            else:
                b.r[en] = tok

    def op(self, en, fn, r=(), w=(), sig=True):
        e = self.eng[en]
        self._waits(en, self._deps(r, w))
        inst = fn()
        if sig:
            e["cnt"] += 1
            inst.then_inc(e["sem"], 1)
            tok = Tok(e["sem"], e["cnt"], en, en)
        else:
            tok = Tok(e["sem"], e["cnt"] + 1, en, en)
        self._commit(tok, en, r, w)
        return inst

    def dma(self, q, out, in_, r=(), w=(), key=None):
        e = self.eng[q]
        self._waits(q, self._deps(r, w))
        if key not in self.dsem:
            self.dsem[key] = [self.nc.alloc_semaphore("dsem_" + key), 0]
        ds = self.dsem[key]
        if ds[1] > 0 and e["waited"].get("dma:" + key, 0) < ds[1]:
            e["h"].wait_ge(ds[0], ds[1])
            e["waited"]["dma:" + key] = ds[1]
        inst = e["h"].dma_start(out=out, in_=in_)
        ds[1] += 16
        inst.then_inc(ds[0], 16)
        tok = Tok(ds[0], ds[1], "dma:" + key, "dma:" + key)
        self._commit(tok, "dma:" + key, r, w)
        return inst

    def finish(self, q="sp"):
        e = self.eng[q]
        for key, (sem, cnt) in self.dsem.items():
            if cnt > 0:
                e["h"].wait_ge(sem, cnt)


def build(ns=4, nt=16, do_sample=True, safe_same=True, dbg=False):
    assert nt % 4 == 0
    nc = bass.Bass("TRN2", target_bir_lowering=False, dynamic_dma_scratch_size=4096)
    C = Sched(nc, safe_same=safe_same)
    V, A, P, T, S = nc.vector, nc.scalar, nc.gpsimd, nc.tensor, nc.sync

    def din(name, shape, dt=F32):
        return nc.dram_tensor(name, list(shape), dt, kind="ExternalInput").ap()

    def dout(name, shape, dt=F32):
        return nc.dram_tensor(name, list(shape), dt, kind="ExternalOutput").ap()

    xp = din("xp", [ns * nt * 128, D])
    xs = din("xs", [DEC_SEQ, D])
    xm = din("xm", [N_META, D])
    w_in_d = din("w_in", [128, KC * NCOL])
    w_out_d = din("w_out", [128, KC * D])
    w1_d = din("w1", [NFC, 128, KC * 256])
    w2_d = din("w2", [4, 128, NFC * 256])
    gT_d = din("gT", [128, 24])
    gf_d = din("gf", [1, D])
    lb_d = din("lbT", [128, 8])
    sink_d = din("sinks", [1, 8])
    cos_d = din("cosT", [128, 18 * 32])
    sin_d = din("sinT", [128, 18 * 32])
    cmk_d = din("cmk", [N_META, 128])
    cmv_d = din("cmv", [N_META, 128])
    cwk_d = din("cwk", [128, 128])
    cwv_d = din("cwv", [128, 128])
    st_d = din("st0", [4, 128, 128])

    yp = dout("yp", [ns * nt * 128, D])
    ys = dout("ys", [DEC_SEQ, D])
    pmk = dout("pmk", [ns, N_META, 128])
    pmv = dout("pmv", [ns, N_META, 128])
    pwk = dout("pwk", [ns, 128, 128])
    pwv = dout("pwv", [ns, 128, 128])
    pst = dout("pst", [ns, 4, 128, 128])
    snk = dout("snk", [DEC_SEQ, 128])
    snv = dout("snv", [DEC_SEQ, 128])
    sst = dout("sst", [4, 128, 128])

    w1s = nc.dram_tensor("w1s", [NFC, 128, KC * 256], BF16, kind="Internal").ap()
    w2s = nc.dram_tensor("w2s", [4, 128, NFC * 256], BF16, kind="Internal").ap()
    b_w1s = [Buf("w1s%d" % i) for i in range(NFC)]
    b_w2s = [Buf("w2s%d" % i) for i in range(4)]

    def sb(name, shape, dt=F32, small=False):
        t = nc.alloc_sbuf_tensor("s_" + name, list(shape), dt).ap()
        return t, Buf(name, small=small)

    def sbn(name, shape, n, dt=F32):
        t = nc.alloc_sbuf_tensor("s_" + name, [shape[0], n] + list(shape[1:]), dt).ap()
        return t, [Buf("%s%d" % (name, i)) for i in range(n)]

    w_in, b_w_in = sb("w_in_sb", [128, KC, NCOL], BF16)
    w_out, b_w_out = sb("w_out_sb", [128, KC, D], BF16)
    w1r, b_w1r = sbn("w1r", [128, KC, 256], NW1, BF16)
    w2r, b_w2r = sbn("w2r", [128, NFC, 256], NW2, BF16)
    h, b_h = sbn("h", [128, D], 8)
    xst, b_xst = sb("xst", [128, D])
    xn, b_xn = sbn("xn", [128, D], 2, BF16)
    xnT, b_xnT = sbn("xnT", [128, KC, 128], 2, BF16)
    hn2T, b_hn2T = sbn("hn2T", [128, KC, 512], 2, BF16)
    actT, b_actT = sb("actT", [128, NFC, 512], BF16)
    sgt, b_sgt = sbn("sgt", [128, 512], 2, BF16)
    tn, b_tn = sb("tn", [128, 512])
    lf, b_lf = sb("lf", [128, 512])
    bb, b_bb = sb("bb", [128, 512])
    enb, b_enb = sb("enb", [128, 512], BF16)
    sq, b_sq = sb("sq", [128, 512], BF16)
    sgg, b_sgg = sbn("sgg", [128, 512], 2, BF16)
    osq, b_osq = sb("osq", [128, 512])
    qT, b_qT = sbn("qT", [128, 512], 2, BF16)
    kfT, b_kfT = sbn("kfT", [128, 512], 2, BF16)
    ktok, b_ktok = sb("ktok", [128, 512], BF16)
    vb, b_vb = sbn("vb", [128, 512], 2, BF16)
    ATb, b_ATb = sb("ATb", [128, 512], BF16)
    Sst, b_S = sb("S", [128, 512])
    Stmp, b_Stmp = lf, b_lf
    Sbf, b_Sbf = sb("Sbf", [128, 512], BF16)
    qf, b_qf = sb("qf", [128, 512])
    oatt, b_oatt = qf, b_qf
    kvf, b_kvf = sb("kvf", [128, 256])
    rt, b_rt = sb("rt", [128, 512])
    Qr, b_Qr = sbn("Qr", [128, 512], 2, BF16)
    QTz, b_QTz = sbn("QTz", [128, 512], 2, BF16)
    Kr, b_Kr = sbn("Kr", [128, 128], 2, BF16)
    kTc, b_kTc = sbn("kTc", [128, 128], 2, BF16)
    kTm, b_kTm = sb("kTm", [128, 128], BF16)
    kTms, b_kTms = sb("kTms", [128, 16], BF16)
    vext, b_vext = sbn("vext", [128, 2, 66], 3, BF16)
    vextm, b_vextm = sb("vextm", [128, 2, 66], BF16)
    vextms, b_vextms = sb("vextms", [16, 2, 66], BF16)
    PTp, b_PTp = sbn("PTp", [128, 512], 2, BF16)
    PTc, b_PTc = sbn("PTc", [128, 512], 2, BF16)
    PTm, b_PTm = sbn("PTm", [128, 512], 2, BF16)
    ldtmp, b_ldtmp = rt[:, 0:256], b_rt
    ebL, b_ebL = sbn("ebL", [128, 4], 2)
    ident, b_ident = sb("ident", [128, 128], BF16)
    maskT, b_maskT = sb("maskT", [128, 128], BF16)
    ones1, b_ones = sb("ones", [128, 2], BF16)
    ones = ones1[:, 0:1].to_broadcast([128, 128])
    gT, b_gT = sb("gT", [128, 24])
    gfb, b_gfb = sb("gfb", [128, D], BF16)
    lbt, b_lbt = sb("lbt", [128, 8])
    cvec, b_cvec = sb("cvec", [128, 4], small=True)
    esink, b_esink = sb("esink", [128, 8], small=True)
    cosT, b_cos = sb("cosT", [128, 18, 32], BF16)
    sinT, b_sin = sb("sinT", [128, 18, 32], BF16)
    mhalf, b_mhalf = sb("mhalf", [128, 4], small=True)
    stat, _ = sb("stat", [128, 48])
    _stat_i = [0]

    def small(name, n):
        i = _stat_i[0]
        _stat_i[0] += n
        assert _stat_i[0] <= 48
        return stat[:, i:i + n], Buf(name, small=True)

    ss1, b_ss1 = small("ss1", 1)
    ms1, b_ms1 = small("ms1", 1)
    rs1, b_rs1 = small("rs1", 1)
    ssh, b_ssh = small("ssh", 4)
    msh, b_msh = small("msh", 4)
    rsh, b_rsh = small("rsh", 4)
    ssa, b_ssa = small("ssa", 1)
    msa, b_msa = small("msa", 1)
    rsa, b_rsa = small("rsa", 1)
    ss2, b_ss2 = small("ss2", 1)
    ms2, b_ms2 = small("ms2", 1)
    rs2, b_rs2 = small("rs2", 1)
    fin_stat = [small("fin%d" % i, 4) for i in range(4)]
    den, b_den = small("den", 4)
    rec, b_rec = small("rec", 4)

    smeta_d = nc.dram_tensor("smeta_d", [128, 512], F32, kind="Internal").ap()
    b_smeta_d = Buf("smeta_d")

    banks = []
    for i in range(8):
        t = nc.alloc_psum_tensor("bank%d" % i, [128, 512], F32).ap()
        banks.append((t, Buf("bank%d" % i, excl=True)))
    mix_rr = [0]

    def nb():
        i = mix_rr[0] % 5
        mix_rr[0] += 1
        assert not banks[i][1].unread, "PSUM bank %d reused before its data was read" % i
        return banks[i]

    ffn_rr = [0]

    def nfb():
        i = 5 + ffn_rr[0] % 3
        ffn_rr[0] += 1
        assert not banks[i][1].unread, "PSUM bank %d reused before its data was read" % i
        return banks[i]

    g1T = gT[:, 0:8]
    g2T = gT[:, 8:16]
    gmT = gT[:, 16:24]

    C.dma("sp", gT, gT_d, w=[b_gT], key="c_gT")
    C.dma("pool", gfb, gf_d.broadcast_to([128, D]), w=[b_gfb], key="c_gf")
    C.dma("sp", lbt, lb_d, w=[b_lbt], key="c_lb")
    C.dma("sp", esink, sink_d.broadcast_to([128, 8]), w=[b_esink], key="c_sink")
    C.dma("pool", cosT, cos_d.rearrange("p (a b) -> p a b", a=18), w=[b_cos], key="c_cos")
    C.dma("pool", sinT, sin_d.rearrange("p (a b) -> p a b", a=18), w=[b_sin], key="c_sin")
    C.dma("pool", w_in, w_in_d.rearrange("p (k n) -> p k n", k=KC), w=[b_w_in], key="w_in")
    C.dma("pool", w_out, w_out_d.rearrange("p (k n) -> p k n", k=KC), w=[b_w_out], key="w_out")
    for i in range(NFC):
        C.dma("pool", w1s[i], w1_d[i], r=[b_w_in, b_w_out], w=[b_w1s[i]], key="w1s%d" % i)
    for i in range(4):
        C.dma("pool", w2s[i], w2_d[i], r=[b_w_in, b_w_out], w=[b_w2s[i]], key="w2s%d" % i)

    C.op("pool", lambda: P.memset(ident, 1.0), w=[b_ident])
    C.op("pool", lambda: P.affine_select(out=ident, in_=ident, pattern=[[-1, 128]], compare_op=ALU.is_equal,
                                         fill=0.0, base=0, channel_multiplier=1), r=[b_ident], w=[b_ident])
    C.op("pool", lambda: P.memset(maskT, 1.0), w=[b_maskT])
    C.op("pool", lambda: P.affine_select(out=maskT, in_=maskT, pattern=[[1, 128]], compare_op=ALU.is_ge,
                                         fill=0.0, base=0, channel_multiplier=-1), r=[b_maskT], w=[b_maskT])
    C.op("pool", lambda: P.memset(ones1, 1.0), w=[b_ones])
    C.op("pool", lambda: P.memset(mhalf, -0.5), w=[b_mhalf])
    C.op("pool", lambda: P.memset(Sst, 0.0), w=[b_S])
    C.op("pool", lambda: P.memset(Sbf, 0.0), w=[b_Sbf])
    for i in range(3):
        C.op("pool", lambda i=i: P.memset(vext[:, i], 1.0), w=[b_vext[i]])
    for i in range(2):
        C.op("pool", lambda i=i: P.memset(PTp[:, i], 0.0), w=[b_PTp[i]])
        C.op("pool", lambda i=i: P.memset(PTc[:, i], 0.0), w=[b_PTc[i]])
    C.op("pool", lambda: P.memset(QTz, 0.0), w=[b_QTz[0], b_QTz[1]])
    C.op("pool", lambda: P.memset(vextm, 1.0), w=[b_vextm])
    C.op("pool", lambda: P.memset(kTm, 0.0), w=[b_kTm])
    C.op("pool", lambda: P.memset(PTm, 0.0), w=[b_PTm[0], b_PTm[1]])
    C.op("pool", lambda: P.memset(vextms, 1.0), w=[b_vextms])
    C.op("dve", lambda: V.tensor_tensor(out=cvec, in0=lbt[:, 0:4], in1=lbt[:, 4:8], op=ALU.subtract),
         r=[b_lbt], w=[b_cvec])
    C.op("act", lambda: A.activation(out=cvec, in_=cvec, func=AF.Exp), r=[b_cvec], w=[b_cvec])
    C.op("dve", lambda: V.tensor_scalar(out=cvec, in0=cvec, scalar1=1.0, scalar2=2.0, op0=ALU.add, op1=ALU.mult),
         r=[b_cvec], w=[b_cvec])
    C.op("dve", lambda: V.reciprocal(out=cvec, in_=cvec), r=[b_cvec], w=[b_cvec])
    C.op("act", lambda: A.activation(out=esink, in_=esink, func=AF.Exp), r=[b_esink], w=[b_esink])

    n_f1_passes = ns * (nt // 4) + (1 if do_sample else 0)
    n_groups_total = ns * (nt // 4) + (1 if do_sample else 0)

    def issue_w1(n):
        if n >= n_f1_passes * NFC:
            return
        slot = n % NW1
        C.dma("sp", w1r[:, slot], w1s[n % NFC].rearrange("p (k n) -> p k n", k=KC),
              r=[b_w1s[n % NFC]], w=[b_w1r[slot]], key="w1r%d" % slot)

    def issue_w2(m):
        if m >= n_groups_total * 4:
            return
        slot = m % NW2
        C.dma("sp", w2r[:, slot], w2s[m % 4].rearrange("p (k n) -> p k n", k=NFC),
              r=[b_w2s[m % 4]], w=[b_w2r[slot]], key="w2r%d" % slot)

    for n in range(NW1):
        issue_w1(n)
    for m in range(NW2):
        issue_w2(m)

    def rstd_chain(ssx, b_ssx, msx, b_msx, rsx, b_rsx, n, inv):
        C.op("pool", lambda: P.tensor_scalar(out=msx, in0=ssx, scalar1=inv, scalar2=EPS, op0=ALU.mult, op1=ALU.add),
             r=[b_ssx], w=[b_msx])
        C.op("pool", lambda: P.tensor_tensor(out=rsx, in0=msx, in1=mhalf[:, 0:n], op=ALU.pow),
             r=[b_msx, b_mhalf], w=[b_rsx])

    def transposes8(src, b_src, dst3, b_dst, gain):
        bk, b_bk = nb()
        bkb = bk.bitcast(BF16)
        for kc in range(KC):
            C.op("pe", lambda kc=kc: T.transpose(out=bkb[:, kc * 128:(kc + 1) * 128],
                                                 in_=src[:, kc * 128:(kc + 1) * 128], identity=ident),
                 r=[b_src, b_ident], w=[b_bk], sig=(kc == KC - 1))
        C.op("dve", lambda: V.tensor_tensor(out=dst3, in0=bkb.rearrange("p (k t) -> p k t", k=KC),
                                            in1=gain.unsqueeze(2).to_broadcast([128, KC, 128]), op=ALU.mult),
             r=[b_bk, b_gT], w=[b_dst])

    def rope(src4, sin_b, cos_b4, tmp4, dst4, r_bufs, b_src, b_tmp, b_dst):
        C.op("pool", lambda: P.tensor_tensor(out=tmp4[:, :, 0, :], in0=src4[:, :, 1, :], in1=sin_b, op=ALU.mult),
             r=[b_src] + r_bufs, w=[b_tmp])
        C.op("pool", lambda: P.tensor_tensor(out=tmp4[:, :, 1, :], in0=src4[:, :, 0, :], in1=sin_b, op=ALU.mult),
             r=[b_src] + r_bufs, w=[b_tmp])
        C.op("pool", lambda: P.tensor_tensor(out=src4, in0=src4, in1=cos_b4, op=ALU.mult),
             r=[b_src, b_tmp] + r_bufs, w=[b_src])
        C.op("pool", lambda: P.tensor_tensor(out=dst4[:, :, 0, :], in0=src4[:, :, 0, :], in1=tmp4[:, :, 0, :],
                                             op=ALU.subtract), r=[b_src, b_tmp], w=[b_dst])
        C.op("pool", lambda: P.tensor_tensor(out=dst4[:, :, 1, :], in0=src4[:, :, 1, :], in1=tmp4[:, :, 1, :],
                                             op=ALU.add), r=[b_src, b_tmp], w=[b_dst])

    def load_x(t):
        nv = t["nvalid"]
        if nv < 128:
            C.op("pool", lambda: P.memset(xst, 0.0), w=[b_xst])
        C.dma("act", xst[0:nv], t["src"], w=[b_xst], key="xst")

    def load_h(t):
        if t["kind"] == "meta":
            return
        hs = t["hslot"]
        nv = t["nvalid"]
        if nv < 128:
            C.op("pool", lambda: P.memset(h[:, hs], 0.0), w=[b_h[hs]])
        C.dma("act", h[0:nv, hs], t["src"], w=[b_h[hs]], key="xh%d" % hs)

    def front(t):
        p = t["p"]
        meta = t["kind"] == "meta"
        L = t["nvalid"]
        ridx = t["ridx"]
        x_t, bx = xst, b_xst
        xn_t, bxn = xn[:, p], b_xn[p]
        xnT_t, bxnT = xnT[:, p], b_xnT[p]

        def norm1():
            C.op("act", lambda: A.activation(out=xn_t, in_=x_t, func=AF.Square, accum_out=ss1), r=[bx],
                 w=[bxn, b_ss1])
            rstd_chain(ss1, b_ss1, ms1, b_ms1, rs1, b_rs1, 1, 1.0 / D)
            C.op("dve", lambda: V.tensor_scalar(out=xn_t, in0=x_t, scalar1=rs1, scalar2=None, op0=ALU.mult),
                 r=[bx, b_rs1], w=[bxn])

        def xnT_():
            transposes8(xn_t, bxn, xnT_t, bxnT, g1T)

        def proj_a(bk, b_bk, c0):
            for hh in range(4):
                for kc in range(KC):
                    C.op("pe", lambda hh=hh, kc=kc: T.matmul(
                        bk[:, hh * 128:(hh + 1) * 128], lhsT=w_in[:, kc, c0 + hh * 128:c0 + (hh + 1) * 128],
                        rhs=xnT_t[:, kc, :], start=(kc == 0), stop=(kc == KC - 1)),
                        r=[b_w_in, bxnT], w=[b_bk], sig=(hh == 3 and kc == KC - 1))

        def proj_b(bk, b_bk, c0, n):
            for kc in range(KC):
                C.op("pe", lambda kc=kc: T.matmul(
                    bk[:, 0:n], lhsT=xnT_t[:, kc, :], rhs=w_in[:, kc, c0:c0 + n],
                    start=(kc == 0), stop=(kc == KC - 1)),
                    r=[b_w_in, bxnT], w=[b_bk], sig=(kc == KC - 1))

        def qf_():
            bkF, b_bkF = nb()
            proj_a(bkF, b_bkF, 512)
            C.op("act", lambda: A.activation(out=tn, in_=bkF, func=AF.Tanh, scale=-0.5), r=[b_bkF], w=[b_tn])
            if not meta:
                bkQ, b_bkQ = nb()
                proj_a(bkQ, b_bkQ, 0)
                C.op("act", lambda: A.activation(out=sq, in_=bkQ, func=AF.Silu), r=[b_bkQ], w=[b_sq])

        def vg_():
            bkV, b_bkV = nb()
            proj_b(bkV, b_bkV, 1024, 512)
            C.op("act", lambda: A.activation(out=vb[:, p], in_=bkV, func=AF.Copy), r=[b_bkV], w=[b_vb[p]])
            if not meta:
                bkG, b_bkG = nb()
                proj_b(bkG, b_bkG, 1536, 512)
                C.op("act", lambda: A.activation(out=sgg[:, p], in_=bkG, func=AF.Silu), r=[b_bkG], w=[b_sgg[p]])

        def kvq_():
            bkKV, b_bkKV = nb()
            proj_b(bkKV, b_bkKV, 2560, 256)
            C.op("act", lambda: A.activation(out=kvf, in_=bkKV[:, 0:256], func=AF.Copy), r=[b_bkKV], w=[b_kvf])
            if not meta:
                bkAQ, b_bkAQ = nb()
                proj_b(bkAQ, b_bkAQ, 2048, 512)
                C.op("act", lambda: A.activation(out=qf, in_=bkAQ, func=AF.Copy), r=[b_bkAQ], w=[b_qf])
            sin_k = sinT[:, ridx, :].unsqueeze(1).to_broadcast([128, 2, 32])
            cos_k4 = cosT[:, ridx, :].unsqueeze(1).unsqueeze(1).to_broadcast([128, 2, 2, 32])
            k4 = kvf[:, 0:128].rearrange("p (a b c) -> p a b c", a=2, b=2)
            rt4k = rt[:, 0:128].rearrange("p (a b c) -> p a b c", a=2, b=2)
            rope(k4, sin_k, cos_k4, rt4k, k4, [b_sin, b_cos], b_kvf, b_rt, b_kvf)
            C.op("pool", lambda: P.tensor_copy(out=Kr[:, p], in_=kvf[:, 0:128]), r=[b_kvf], w=[b_Kr[p]])
            outs = t["outs"]
            if "k" in outs:
                for i, (dst, nrow) in enumerate(outs["k"]):
                    C.dma("sp", dst, kvf[0:nrow, 0:128], r=[b_kvf], key="o_k%d" % i)
                for i, (dst, nrow) in enumerate(outs["v"]):
                    C.dma("sp", dst, kvf[0:nrow, 128:256], r=[b_kvf], key="o_v%d" % i)
            if meta:
                C.op("act", lambda: A.activation(out=vextm[0:16, :, 0:64],
                                                 in_=kvf[0:16, 128:256].rearrange("p (a d) -> p a d", a=2),
                                                 func=AF.Copy), r=[b_kvf], w=[b_vextm])
            else:
                C.op("act", lambda: A.activation(out=vext[:, t["v3"], :, 0:64],
                                                 in_=kvf[:, 128:256].rearrange("p (a d) -> p a d", a=2),
                                                 func=AF.Copy), r=[b_kvf], w=[b_vext[t["v3"]]])
                sin_q = sinT[:, ridx, :].unsqueeze(1).to_broadcast([128, 8, 32])
                cos_q4 = cosT[:, ridx, :].unsqueeze(1).unsqueeze(1).to_broadcast([128, 8, 2, 32])
                q4 = qf.rearrange("p (a b c) -> p a b c", a=8, b=2)
                rt4 = rt.rearrange("p (a b c) -> p a b c", a=8, b=2)
                Qr4 = Qr[:, p].rearrange("p (a b c) -> p a b c", a=8, b=2)
                rope(q4, sin_q, cos_q4, rt4, Qr4, [b_sin, b_cos], b_qf, b_rt, b_Qr[p])

        def gates1():
            for hh in range(4):
                C.op("act", lambda hh=hh: A.activation(out=tn[:, hh * 128:(hh + 1) * 128],
                                                       in_=tn[:, hh * 128:(hh + 1) * 128], func=AF.Identity,
                                                       scale=cvec[:, hh:hh + 1], bias=cvec[:, hh:hh + 1]),
                     r=[b_tn, b_cvec], w=[b_tn])
            C.op("act", lambda: A.activation(out=lf, in_=tn, func=AF.Ln, scale=-1.0, bias=1.0), r=[b_tn], w=[b_lf])

        def gates2():
            for hh in range(4):
                C.op("dve", lambda hh=hh: V.tensor_tensor_scan(
                    out=bb[:, hh * 128:(hh + 1) * 128], data0=ones, data1=lf[:, hh * 128:(hh + 1) * 128],
                    initial=0.0, op0=ALU.mult, op1=ALU.add), r=[b_lf, b_ones], w=[b_bb])

        def gates3():
            C.op("act", lambda: A.activation(out=enb, in_=bb, func=AF.Exp, scale=-1.0), r=[b_bb], w=[b_enb])
            C.op("act", lambda: A.activation(out=bb, in_=bb, func=AF.Exp), r=[b_bb], w=[b_bb])

        def gates4():
            if not meta:
                C.op("dve", lambda: V.tensor_tensor(out=qT[:, p], in0=sq, in1=bb, op=ALU.mult),
                     r=[b_sq, b_bb], w=[b_qT[p]])
            C.op("dve", lambda: V.tensor_tensor(out=kfT[:, p], in0=tn, in1=enb, op=ALU.mult),
                 r=[b_tn, b_enb], w=[b_kfT[p]])
            C.op("dve", lambda: V.tensor_copy(out=ebL[:, p], in_=bb.rearrange("p (a t) -> p a t", a=4)[:, :, L - 1]),
                 r=[b_bb], w=[b_ebL[p]])

        return {"norm1": norm1, "xnT": xnT_, "qf": qf_, "vg": vg_, "kvq": kvq_, "gates1": gates1,
                "gates2": gates2, "gates3": gates3, "gates4": gates4}

    def back(t):
        p = t["p"]
        pp = 1 - p
        kind = t["kind"]
        meta = kind == "meta"
        sample = kind == "sample"
        L = t["nvalid"]
        hslot = t["hslot"]
        has_prev = t["has_prev"]
        outs = t["outs"]
        kTm_use, b_kTm_use = (kTms, b_kTms) if sample else (kTm, b_kTm)
        vxm_use, b_vxm_use = (vextms, b_vextms) if sample else (vextm, b_vextm)
        ncur = L if sample else 128
        xn_t, bxn = xn[:, p], b_xn[p]
        mixedT, b_mixedT = xnT[:, p], b_xnT[p]
        mixed, b_mixed = xn_t, bxn
        hsl = (t["grp"] or 0) % 2
        vc = t["v3"]
        vp = (vc + 2) % 3

        def init():
            if t.get("init_state") == "meta":
                C.dma("sp", Sst, smeta_d, r=[b_smeta_d], w=[b_S], key="ld_sm")
                C.op("act", lambda: A.activation(out=Sbf, in_=Sst, func=AF.Copy), r=[b_S], w=[b_Sbf])
            elif t.get("init_state") == "cache":
                C.dma("sp", Sst.rearrange("p (a v) -> p a v", a=4), st_d.rearrange("a k v -> k a v"), w=[b_S],
                      key="ld_s")
                C.op("act", lambda: A.activation(out=Sbf, in_=Sst, func=AF.Copy), r=[b_S], w=[b_Sbf])

        def qk_tr():
            bkT, b_bkT = nb()
            bkTb = bkT.bitcast(BF16)
            if not meta:
                for t4 in range(4):
                    C.op("pe", lambda t4=t4: T.transpose(out=bkTb[:, t4 * 128:(t4 + 1) * 128],
                                                         in_=Qr[:, p, t4 * 128:(t4 + 1) * 128], identity=ident),
                         r=[b_Qr[p], b_ident], w=[b_bkT], sig=False)
            C.op("pe", lambda: T.transpose(out=bkTb[:, 512:640], in_=Kr[:, p], identity=ident),
                 r=[b_Kr[p], b_ident], w=[b_bkT])
            if meta:
                C.op("dve", lambda: V.tensor_copy(out=kTm[:, 0:16], in_=bkTb[:, 512:528]), r=[b_bkT], w=[b_kTm])
            else:
                C.op("dve", lambda: V.tensor_copy(out=QTz[0:64, 0], in_=bkTb[0:64, 0:512]), r=[b_bkT], w=[b_QTz[0]])
                C.op("dve", lambda: V.tensor_copy(out=QTz[64:128, 1], in_=bkTb[64:128, 0:512]), r=[b_bkT],
                     w=[b_QTz[1]])
                C.op("dve", lambda: V.tensor_copy(out=kTc[:, p], in_=bkTb[:, 512:640]), r=[b_bkT], w=[b_kTc[p]])

        def hgrn_a():
            if not meta:
                bkA, b_bkA = nb()
                for hh in range(4):
                    C.op("pe", lambda hh=hh: T.matmul(bkA[:, hh * 128:(hh + 1) * 128],
                                                      lhsT=kfT[:, p, hh * 128:(hh + 1) * 128],
                                                      rhs=qT[:, p, hh * 128:(hh + 1) * 128], start=True, stop=True),
                         r=[b_kfT[p], b_qT[p]], w=[b_bkA], sig=(hh == 3))
                C.op("dve", lambda: V.tensor_tensor(out=ATb.rearrange("p (a t) -> p a t", a=4),
                                                    in0=bkA.rearrange("p (a t) -> p a t", a=4),
                                                    in1=maskT.unsqueeze(1).to_broadcast([128, 4, 128]),
                                                    op=ALU.mult), r=[b_bkA, b_maskT], w=[b_ATb])
            bkTk, b_bkTk = nb()
            bkTkb = bkTk.bitcast(BF16)
            for hh in range(4):
                C.op("pe", lambda hh=hh: T.transpose(out=bkTkb[:, hh * 128:(hh + 1) * 128],
                                                     in_=kfT[:, p, hh * 128:(hh + 1) * 128], identity=ident),
                     r=[b_kfT[p], b_ident], w=[b_bkTk], sig=(hh == 3))
            C.op("act", lambda: A.activation(out=ktok, in_=bkTkb[:, 0:512], func=AF.Copy), r=[b_bkTk], w=[b_ktok])

        def scores(g):
            if meta:
                return
            if has_prev:
                bkX, b_bkX = nb()
                C.op("pe", lambda: T.matmul(bkX, lhsT=kTc[:, pp, :], rhs=QTz[:, g], start=True, stop=True),
                     r=[b_kTc[pp], b_QTz[g]], w=[b_bkX])
                if sample:
                    C.op("act", lambda: A.activation(out=PTp[:, g], in_=bkX, func=AF.Exp, scale=0.125),
                         r=[b_bkX], w=[b_PTp[g]])
                else:
                    C.op("act", lambda: A.activation(out=PTp[64:128, g], in_=bkX[64:128, :], func=AF.Exp,
                                                     scale=0.125), r=[b_bkX], w=[b_PTp[g]])
                    C.op("act", lambda: A.activation(
                        out=PTp[0:64, g].rearrange("p (a t) -> p a t", a=4)[:, :, 0:64],
                        in_=bkX[0:64, :].rearrange("p (a t) -> p a t", a=4)[:, :, 0:64], func=AF.Exp, scale=0.125),
                        r=[b_bkX], w=[b_PTp[g]])
            bkY, b_bkY = nb()
            C.op("pe", lambda: T.matmul(bkY[0:ncur, :], lhsT=kTc[:, p, 0:ncur], rhs=QTz[:, g], start=True,
                                        stop=True), r=[b_kTc[p], b_QTz[g]], w=[b_bkY])
            if sample:
                C.op("act", lambda: A.activation(out=PTc[0:ncur, g], in_=bkY[0:ncur, :], func=AF.Exp, scale=0.125),
                     r=[b_bkY], w=[b_PTc[g]])
            else:
                C.op("act", lambda: A.activation(out=PTc[0:64, g], in_=bkY[0:64, :], func=AF.Exp, scale=0.125),
                     r=[b_bkY], w=[b_PTc[g]])
                C.op("act", lambda: A.activation(
                    out=PTc[64:128, g].rearrange("p (a t) -> p a t", a=4)[:, :, 64:128],
                    in_=bkY[64:128, :].rearrange("p (a t) -> p a t", a=4)[:, :, 64:128], func=AF.Exp, scale=0.125),
                    r=[b_bkY], w=[b_PTc[g]])
            bkM, b_bkM = nb()
            if sample:
                C.op("pe", lambda: T.matmul(bkM[0:16, :], lhsT=kTm_use, rhs=QTz[:, g], start=True, stop=True),
                     r=[b_kTm_use, b_QTz[g]], w=[b_bkM])
            else:
                C.op("pe", lambda: T.matmul(bkM, lhsT=kTm_use, rhs=QTz[:, g], start=True, stop=True),
                     r=[b_kTm_use, b_QTz[g]], w=[b_bkM])
            C.op("act", lambda: A.activation(out=PTm[0:16, g], in_=bkM[0:16, :], func=AF.Exp, scale=0.125),
                 r=[b_bkM], w=[b_PTm[g]])

        def pv(g):
            if meta:
                return
            bkP, b_bkP = nb()
            for r4 in range(4):
                terms = []
                if has_prev:
                    terms.append((PTp[:, g, r4 * 128:(r4 + 1) * 128], vext[:, vp, g, 0:65], [b_PTp[g], b_vext[vp]]))
                terms.append((PTc[0:ncur, g, r4 * 128:(r4 + 1) * 128], vext[0:ncur, vc, g, 0:65],
                              [b_PTc[g], b_vext[vc]]))
                if sample:
                    terms.append((PTm[0:16, g, r4 * 128:(r4 + 1) * 128], vxm_use[:, g, 0:65],
                                  [b_PTm[g], b_vxm_use]))
                else:
                    terms.append((PTm[:, g, r4 * 128:(r4 + 1) * 128], vxm_use[:, g, 0:65], [b_PTm[g], b_vxm_use]))
                for ti, (lt, rh, rb) in enumerate(terms):
                    C.op("pe", lambda lt=lt, rh=rh, ti=ti, r4=r4: T.matmul(
                        bkP[:, r4 * 65:(r4 + 1) * 65], lhsT=lt, rhs=rh, start=(ti == 0),
                        stop=(ti == len(terms) - 1)), r=rb, w=[b_bkP],
                        sig=(r4 == 3 and ti == len(terms) - 1))
            bkP3 = bkP[:, 0:260].rearrange("p (a d) -> p a d", a=4)
            C.op("dve", lambda: V.tensor_tensor(out=den, in0=bkP3[:, :, 64], in1=esink[:, g * 4:(g + 1) * 4],
                                                op=ALU.add), r=[b_bkP, b_esink], w=[b_den])
            C.op("dve", lambda: V.reciprocal(out=rec, in_=den), r=[b_den], w=[b_rec])
            C.op("dve", lambda: V.tensor_tensor(
                out=oatt[:, g * 256:(g + 1) * 256].rearrange("p (a d) -> p a d", a=4), in0=bkP3[:, :, 0:64],
                in1=rec.unsqueeze(2).to_broadcast([128, 4, 64]), op=ALU.mult), r=[b_bkP, b_rec], w=[b_oatt])
            if g == 1:
                C.op("act", lambda: A.activation(out=osq, in_=oatt, func=AF.Square, accum_out=ssa),
                     r=[b_oatt], w=[b_osq, b_ssa])
                rstd_chain(ssa, b_ssa, msa, b_msa, rsa, b_rsa, 1, 1.0 / 512)
                C.op("act", lambda: A.activation(out=mixed[:, 512:1024], in_=oatt, func=AF.Copy, scale=rsa),
                     r=[b_oatt, b_rsa], w=[b_mixed])

        def hgrn_o():
            if not meta:
                bkO, b_bkO = nb()
                for hh in range(4):
                    C.op("pe", lambda hh=hh: T.matmul(bkO[:, hh * 128:(hh + 1) * 128],
                                                      lhsT=qT[:, p, hh * 128:(hh + 1) * 128],
                                                      rhs=Sbf[:, hh * 128:(hh + 1) * 128], start=True, stop=False),
                         r=[b_qT[p], b_Sbf], w=[b_bkO], sig=False)
                    C.op("pe", lambda hh=hh: T.matmul(bkO[:, hh * 128:(hh + 1) * 128],
                                                      lhsT=ATb[:, hh * 128:(hh + 1) * 128],
                                                      rhs=vb[:, p, hh * 128:(hh + 1) * 128], start=False, stop=True),
                         r=[b_ATb, b_vb[p]], w=[b_bkO], sig=(hh == 3))
                C.op("act", lambda: A.activation(out=osq, in_=bkO, func=AF.Square), r=[b_bkO], w=[b_osq])
                C.op("dve", lambda: V.tensor_reduce(out=ssh, in_=osq.rearrange("p (a t) -> p a t", a=4),
                                                    axis=AX.X, op=ALU.add), r=[b_osq], w=[b_ssh])
                rstd_chain(ssh, b_ssh, msh, b_msh, rsh, b_rsh, 4, 1.0 / 128)
                C.op("dve", lambda: V.tensor_tensor(out=osq, in0=bkO, in1=sgg[:, p], op=ALU.mult),
                     r=[b_bkO, b_sgg[p]], w=[b_osq])
                C.op("dve", lambda: V.tensor_tensor(out=mixed[:, 0:512].rearrange("p (a t) -> p a t", a=4),
                                                    in0=osq.rearrange("p (a t) -> p a t", a=4),
                                                    in1=rsh.unsqueeze(2).to_broadcast([128, 4, 128]), op=ALU.mult),
                     r=[b_osq, b_rsh], w=[b_mixed])
            bkS, b_bkS = nb()
            for hh in range(4):
                C.op("pe", lambda hh=hh: T.matmul(bkS[:, hh * 128:(hh + 1) * 128],
                                                  lhsT=ktok[:, hh * 128:(hh + 1) * 128],
                                                  rhs=vb[:, p, hh * 128:(hh + 1) * 128], start=True, stop=True),
                     r=[b_ktok, b_vb[p]], w=[b_bkS], sig=(hh == 3))
            C.op("dve", lambda: V.tensor_tensor(out=Stmp, in0=bkS, in1=Sst, op=ALU.add), r=[b_bkS, b_S], w=[b_Stmp])
            C.op("dve", lambda: V.tensor_tensor(out=Sst.rearrange("p (a t) -> p a t", a=4),
                                                in0=Stmp.rearrange("p (a t) -> p a t", a=4),
                                                in1=ebL[:, p].unsqueeze(2).to_broadcast([128, 4, 128]), op=ALU.mult),
                 r=[b_Stmp, b_ebL[p]], w=[b_S])
            C.op("act", lambda: A.activation(out=Sbf, in_=Sst, func=AF.Copy), r=[b_S], w=[b_Sbf])
            if meta:
                C.dma("sp", smeta_d, Sst, r=[b_S], w=[b_smeta_d], key="st_sm")
            if "state" in outs:
                C.dma("sp", outs["state"].rearrange("a k v -> k a v"), Sst.rearrange("p (a v) -> p a v", a=4),
                      r=[b_S], key="o_state")

        def mixT():
            if meta:
                return
            transposes8(mixed, b_mixed, mixedT, b_mixedT, gmT)

        def oproj():
            if meta:
                return
            for cg in range(2):
                bkC, b_bkC = nb()
                for kc in range(KC):
                    C.op("pe", lambda kc=kc: T.matmul(bkC, lhsT=mixedT[:, kc, :],
                                                      rhs=w_out[:, kc, cg * 512:(cg + 1) * 512],
                                                      start=(kc == 0), stop=(kc == KC - 1)),
                         r=[b_mixedT, b_w_out], w=[b_bkC], sig=(kc == KC - 1))
                C.op("dve", lambda: V.tensor_tensor(out=h[:, hslot, cg * 512:(cg + 1) * 512], in0=bkC,
                                                    in1=h[:, hslot, cg * 512:(cg + 1) * 512], op=ALU.add),
                     r=[b_bkC, b_h[hslot]], w=[b_h[hslot]])
            C.op("act", lambda: A.activation(out=xn_t, in_=h[:, hslot], func=AF.Square, accum_out=ss2),
                 r=[b_h[hslot]], w=[bxn, b_ss2])
            rstd_chain(ss2, b_ss2, ms2, b_ms2, rs2, b_rs2, 1, 1.0 / D)
            C.op("dve", lambda: V.tensor_scalar(out=xn_t, in0=h[:, hslot], scalar1=rs2, scalar2=None, op0=ALU.mult),
                 r=[b_h[hslot], b_rs2], w=[bxn])

        def hn2T_():
            if meta:
                return
            col = t["hn2_col"]
            transposes8(xn_t, bxn, hn2T[:, hsl, :, col * 128:(col + 1) * 128], b_hn2T[hsl], g2T)

        return {"init": init, "qk_tr": qk_tr, "scores0": lambda: scores(0), "scores1": lambda: scores(1),
                "hgrn_a": hgrn_a, "pv0": lambda: pv(0), "pv1": lambda: pv(1), "hgrn_o": hgrn_o,
                "mixT": mixT, "oproj": oproj, "hn2T": hn2T_}

    w1_ctr = [0]
    w2_ctr = [0]

    def f1_steps(hsl, N):
        c0 = 0
        steps = []
        for fc in range(NFC):
            def step(fc=fc):
                n = w1_ctr[0]
                w1_ctr[0] += 1
                slot = n % NW1
                bkG, b_bkG = nfb()
                bkU, b_bkU = nfb()
                for (bk, b_bk, cc) in ((bkG, b_bkG, 0), (bkU, b_bkU, 128)):
                    for kc in range(KC):
                        C.op("pe", lambda bk=bk, kc=kc, cc=cc: T.matmul(
                            bk[:, 0:N], lhsT=w1r[:, slot, kc, cc:cc + 128], rhs=hn2T[:, hsl, kc, c0:c0 + N],
                            start=(kc == 0), stop=(kc == KC - 1)),
                            r=[b_w1r[slot], b_hn2T[hsl]], w=[b_bk], sig=(kc == KC - 1))
                issue_w1(n + NW1)
                ss = n % 2
                C.op("act", lambda: A.activation(out=sgt[:, ss, 0:N], in_=bkG[:, 0:N], func=AF.Silu),
                     r=[b_bkG], w=[b_sgt[ss]])
                C.op("dve", lambda: V.tensor_tensor(out=actT[:, fc, c0:c0 + N], in0=bkU[:, 0:N], in1=sgt[:, ss, 0:N],
                                                    op=ALU.mult), r=[b_bkU, b_sgt[ss]], w=[b_actT])
            steps.append(step)
        return steps

    def f2_steps(hslots):
        steps = []
        for q in range(4):
            for st, hs in enumerate(hslots):
                def step(q=q, st=st, hs=hs):
                    if st == 0:
                        w2_ctr[0] += 1
                    m = w2_ctr[0] - 1
                    slot = m % NW2
                    bkF, b_bkF = nfb()
                    for fc in range(NFC):
                        C.op("pe", lambda fc=fc: T.matmul(bkF[:, 0:256], lhsT=actT[:, fc, st * 128:(st + 1) * 128],
                                                          rhs=w2r[:, slot, fc, :], start=(fc == 0),
                                                          stop=(fc == NFC - 1)),
                             r=[b_actT, b_w2r[slot]], w=[b_bkF], sig=(fc == NFC - 1))
                    C.op("dve", lambda: V.tensor_tensor(out=h[:, hs, q * 256:(q + 1) * 256], in0=bkF[:, 0:256],
                                                        in1=h[:, hs, q * 256:(q + 1) * 256], op=ALU.add),
                         r=[b_bkF, b_h[hs]], w=[b_h[hs]])
                    if st == len(hslots) - 1:
                        issue_w2(m + NW2)
                steps.append(step)
        return steps

    def fin_steps(hslots, y_dsts):
        stepsA, stepsB = [], []
        for st, hs in enumerate(hslots):
            sA, b_sA = fin_stat[st % 4]

            def stepA(hs=hs, sA=sA, b_sA=b_sA):
                C.op("act", lambda: A.activation(out=osq, in_=h[:, hs, 0:512], func=AF.Square, accum_out=sA[:, 0:1]),
                     r=[b_h[hs]], w=[b_osq, b_sA])
                C.op("act", lambda: A.activation(out=osq, in_=h[:, hs, 512:1024], func=AF.Square,
                                                 accum_out=sA[:, 1:2]), r=[b_h[hs]], w=[b_osq, b_sA])
                C.op("pool", lambda: P.tensor_tensor(out=sA[:, 0:1], in0=sA[:, 0:1], in1=sA[:, 1:2], op=ALU.add),
                     r=[b_sA], w=[b_sA])
                C.op("pool", lambda: P.tensor_scalar(out=sA[:, 2:3], in0=sA[:, 0:1], scalar1=1.0 / D, scalar2=EPS,
                                                     op0=ALU.mult, op1=ALU.add), r=[b_sA], w=[b_sA])
                C.op("pool", lambda: P.tensor_tensor(out=sA[:, 3:4], in0=sA[:, 2:3], in1=mhalf[:, 0:1], op=ALU.pow),
                     r=[b_sA, b_mhalf], w=[b_sA])

            def stepB(st=st, hs=hs, sA=sA, b_sA=b_sA):
                C.op("dve", lambda: V.scalar_tensor_tensor(out=h[:, hs], in0=h[:, hs], scalar=sA[:, 3:4], in1=gfb,
                                                           op0=ALU.mult, op1=ALU.mult),
                     r=[b_h[hs], b_sA, b_gfb], w=[b_h[hs]])
                dst, nrow = y_dsts[st]
                C.dma("sp", dst, h[0:nrow, hs], r=[b_h[hs]], key="o_y%d" % hs)
            stepsA.append(stepA)
            stepsB.append(stepB)
        return stepsA + stepsB

    tiles = []
    tiles.append(dict(kind="meta", src=xm, nvalid=N_META, ridx=17, has_prev=False, hn2_col=0,
                      outs={"k": [(pmk[s], N_META) for s in range(ns)], "v": [(pmv[s], N_META) for s in range(ns)]},
                      grp=None))
    gidx = 0
    if do_sample:
        tiles.append(dict(kind="sample", src=xs, nvalid=DEC_SEQ, ridx=16, has_prev=True, hn2_col=0,
                          outs={"k": [(snk, DEC_SEQ)], "v": [(snv, DEC_SEQ)], "state": sst},
                          init_state="cache", grp=gidx, gpos=0, ydst=(ys, DEC_SEQ)))
        gidx += 1
    for s in range(ns):
        for j in range(nt):
            row0 = (s * nt + j) * 128
            outs = {}
            if j == nt - 1:
                outs = {"k": [(pwk[s], 128)], "v": [(pwv[s], 128)], "state": pst[s]}
            tl = dict(kind="prompt", src=xp[row0:row0 + 128, :], nvalid=128, ridx=j, has_prev=j > 0,
                      hn2_col=j % 4, outs=outs, grp=gidx, gpos=j % 4,
                      ydst=(yp[row0:row0 + 128, :], 128))
            if j == 0:
                tl["init_state"] = "meta"
            if j % 4 == 3:
                gidx += 1
            tiles.append(tl)
    for i, tl in enumerate(tiles):
        tl["p"] = i % 2
        tl["v3"] = i % 3
        tl["hslot"] = i % 8

    def sample_prep(p_prev, v_prev):
        C.dma("sp", ldtmp[:, 0:128], cwk_d, w=[b_ldtmp], key="ld_c0")
        C.dma("sp", ldtmp[:, 128:256], cwv_d, w=[b_ldtmp], key="ld_c1")
        C.op("pool", lambda: P.tensor_copy(out=Kr[:, p_prev], in_=ldtmp[:, 0:128]), r=[b_ldtmp], w=[b_Kr[p_prev]])
        bk, b_bk = nb()
        bkb = bk.bitcast(BF16)
        C.op("pe", lambda: T.transpose(out=bkb[:, 0:128], in_=Kr[:, p_prev], identity=ident),
             r=[b_Kr[p_prev], b_ident], w=[b_bk])
        C.op("dve", lambda: V.tensor_copy(out=kTc[:, p_prev], in_=bkb[:, 0:128]), r=[b_bk], w=[b_kTc[p_prev]])
        C.op("act", lambda: A.activation(out=vext[:, v_prev, :, 0:64],
                                         in_=ldtmp[:, 128:256].rearrange("p (a d) -> p a d", a=2), func=AF.Copy),
             r=[b_ldtmp], w=[b_vext[v_prev]])
        C.dma("sp", ldtmp[0:16, 0:128], cmk_d, w=[b_ldtmp], key="ld_c0")
        C.dma("sp", ldtmp[0:16, 128:256], cmv_d, w=[b_ldtmp], key="ld_c1")
        C.op("pool", lambda: P.tensor_copy(out=Kr[:, p_prev], in_=ldtmp[:, 0:128]), r=[b_ldtmp], w=[b_Kr[p_prev]])
        bk, b_bk = nb()
        bkb = bk.bitcast(BF16)
        C.op("pe", lambda: T.transpose(out=bkb[:, 0:128], in_=Kr[:, p_prev], identity=ident),
             r=[b_Kr[p_prev], b_ident], w=[b_bk])
        C.op("dve", lambda: V.tensor_copy(out=kTms, in_=bkb[:, 0:16]), r=[b_bk], w=[b_kTms])
        C.op("act", lambda: A.activation(out=vextms[:, :, 0:64],
                                         in_=ldtmp[0:16, 128:256].rearrange("p (a d) -> p a d", a=2), func=AF.Copy),
             r=[b_ldtmp], w=[b_vextms])

    ORDER = ["F.norm1", "B.init", "B.qk_tr", "FILL1", "F.xnT", "B.scores0", "B.scores1", "FILL1", "F.kvq",
             "B.hgrn_a", "FILL1", "B.pv0", "B.pv1", "FILL2", "B.hgrn_o", "FILL1", "F.qf", "F.gates1", "B.mixT",
             "FILL2", "F.gates2", "B.oproj", "FILL2", "F.gates3", "F.vg", "FILL1", "F.gates4", "B.hn2T"]

    if ORDER_OVERRIDE is not None:
        ORDER = list(ORDER_OVERRIDE)

    from collections import deque
    fillq = deque()

    def pop_fill(k):
        for _ in range(k):
            if not fillq:
                return
            fillq.popleft()[1]()

    def drain_through(grp, kinds):
        last = -1
        for idx, (tag, _) in enumerate(fillq):
            if tag[0] < grp or (tag[0] == grp and tag[1] in kinds):
                last = idx
        pop_fill(last + 1)

    def group_tiles(g):
        return [tl for tl in tiles if tl["grp"] == g]

    load_x(tiles[0])
    for i in range(len(tiles) + 1):
        bt = tiles[i - 1] if i >= 1 else None
        ft = tiles[i] if i < len(tiles) else None
        if bt is not None and bt["kind"] == "sample":
            sample_prep(1 - bt["p"], (bt["v3"] + 2) % 3)
        if bt is not None:
            if bt["grp"] is not None and bt["grp"] >= 2:
                drain_through(bt["grp"] - 2, ("A", "B", "F2", "fin"))
            load_h(bt)
        Fd = front(ft) if ft is not None else {}
        Bd = back(bt) if bt is not None else {}
        for name in ORDER:
            if name.startswith("FILL"):
                pop_fill(int(name[4:]))
                continue
            if name == "B.hn2T" and bt is not None and bt["grp"] is not None and bt["grp"] >= 2:
                drain_through(bt["grp"] - 2, ("A",))
            d = Fd if name[0] == "F" else Bd
            fn = d.get(name[2:])
            if fn is not None:
                fn()
            if name == "F.norm1" and i + 1 < len(tiles):
                load_x(tiles[i + 1])
        if bt is not None and bt["kind"] == "sample":
            for g2 in range(2):
                C.op("pool", lambda g2=g2: P.memset(PTp[0:64, g2], 0.0), w=[b_PTp[g2]])
                C.op("pool", lambda g2=g2: P.memset(PTc[64:128, g2], 0.0), w=[b_PTc[g2]])
        if bt is not None and bt["grp"] is not None:
            g = bt["grp"]
            gt = group_tiles(g)
            if bt["kind"] == "sample":
                for st in f1_steps(g % 2, 128):
                    fillq.append(((g, "A"), st))
                for st in f2_steps([bt["hslot"]]):
                    fillq.append(((g, "F2"), st))
                for st in fin_steps([bt["hslot"]], [bt["ydst"]]):
                    fillq.append(((g, "fin"), st))
            elif bt["gpos"] == 3:
                for st in f1_steps(g % 2, 512):
                    fillq.append(((g, "A"), st))
                hs = [tl["hslot"] for tl in gt]
                for st in f2_steps(hs):
                    fillq.append(((g, "F2"), st))
                for st in fin_steps(hs, [tl["ydst"] for tl in gt]):
                    fillq.append(((g, "fin"), st))
    pop_fill(len(fillq))
    C.finish("sp")
    return nc


def _rope_tables():
    inv = 10000.0 ** (-np.arange(0, 64, 2, dtype=np.float64) / 64.0)
    pos = np.zeros((128, 18), np.float64)
    for j in range(16):
        pos[:, j] = N_META + j * 128 + np.arange(128)
    pos[:, 16] = N_META + PAST_LEN + np.arange(128)
    pos[:, 17] = np.arange(128)
    ang = pos[:, :, None] * inv[None, None, :]
    return (np.cos(ang).astype(np.float32).reshape(128, 18 * 32),
            np.sin(ang).astype(np.float32).reshape(128, 18 * 32))


def _layout_shared(w_in, w_out, w_ffn_in, w_ffn_out, norm1, norm2, hg_norm, attn_norm, final_norm, lb_param,
                   attn_sinks, meta_tokens):
    perm = list(range(2048))
    for t in range(4):
        perm += list(range(2048 + t * 64, 2048 + (t + 1) * 64))
        perm += list(range(2048 + (4 + t) * 64, 2048 + (5 + t) * 64))
    perm += list(range(2560, 2816))
    wi = np.ascontiguousarray(w_in[0][:, perm].reshape(KC, 128, NCOL).transpose(1, 0, 2).reshape(128, KC * NCOL))
    wo = np.ascontiguousarray(w_out[0].reshape(KC, 128, D).transpose(1, 0, 2).reshape(128, KC * D))
    w1 = w_ffn_in[0].reshape(KC, 128, 2, NFC, 128)
    w1 = np.ascontiguousarray(w1.transpose(3, 1, 0, 2, 4).reshape(NFC, 128, KC * 256))
    w2 = w_ffn_out[0].reshape(NFC, 128, 4, 256)
    w2 = np.ascontiguousarray(w2.transpose(2, 1, 0, 3).reshape(4, 128, NFC * 256))
    gm = np.concatenate([hg_norm[0], attn_norm[0]])
    gT = np.ascontiguousarray(np.concatenate([norm1[0].reshape(KC, 128).T, norm2[0].reshape(KC, 128).T,
                                              gm.reshape(KC, 128).T], axis=1))
    lbT = np.ascontiguousarray(lb_param.reshape(2, 4, 128).transpose(2, 0, 1).reshape(128, 8))
    cosT, sinT = _rope_tables()
    return {"w_in": wi, "w_out": wo, "w1": w1, "w2": w2, "gT": gT, "gf": np.ascontiguousarray(final_norm.reshape(1, D)),
            "lbT": lbT, "sinks": np.ascontiguousarray(attn_sinks.reshape(1, 8)), "cosT": cosT, "sinT": sinT,
            "xm": np.ascontiguousarray(meta_tokens)}


_NC_CACHE = {}


def kernel(x_prompt, x_sample, cache_meta_k, cache_meta_v, cache_win_k, cache_win_v, state_hgrn,
           meta_tokens, norm1, w_in, lb_param, hg_norm, attn_sinks, attn_norm, w_out, norm2,
           w_ffn_in, w_ffn_out, final_norm):
    f = lambda a: np.asarray(a, dtype=np.float32)
    (x_prompt, x_sample, cache_meta_k, cache_meta_v, cache_win_k, cache_win_v, state_hgrn, meta_tokens, norm1,
     w_in, lb_param, hg_norm, attn_sinks, attn_norm, w_out, norm2, w_ffn_in, w_ffn_out, final_norm) = map(
        f, (x_prompt, x_sample, cache_meta_k, cache_meta_v, cache_win_k, cache_win_v, state_hgrn, meta_tokens,
            norm1, w_in, lb_param, hg_norm, attn_sinks, attn_norm, w_out, norm2, w_ffn_in, w_ffn_out, final_norm))
    ns, nt = 4, 16
    if "nc" not in _NC_CACHE:
        _NC_CACHE["nc"] = build(ns, nt, True)
    nc = _NC_CACHE["nc"]
    shared = _layout_shared(w_in, w_out, w_ffn_in, w_ffn_out, norm1, norm2, hg_norm, attn_norm, final_norm,
                            lb_param, attn_sinks, meta_tokens)
    in_maps = []
    for c in range(N_CORES):
        m = dict(shared)
        m["xp"] = np.ascontiguousarray(x_prompt[c * ns:(c + 1) * ns].reshape(ns * SEQ, D))
        m["xs"] = np.ascontiguousarray(x_sample[c])
        m["cmk"] = np.ascontiguousarray(cache_meta_k[0, c].reshape(N_META, 128))
        m["cmv"] = np.ascontiguousarray(cache_meta_v[0, c].reshape(N_META, 128))
        m["cwk"] = np.ascontiguousarray(cache_win_k[0, c].reshape(128, 128))
        m["cwv"] = np.ascontiguousarray(cache_win_v[0, c].reshape(128, 128))
        m["st0"] = np.ascontiguousarray(state_hgrn[0, c])
        in_maps.append(m)
    res = run_bass_kernel_spmd(nc, in_maps, core_ids=list(range(N_CORES)))
    R = res.results
    cat = lambda k: np.concatenate([np.asarray(r[k]) for r in R], axis=0)
    y_prompt = cat("yp").reshape(32, SEQ, D)
    y_sample = np.stack([np.asarray(r["ys"]) for r in R], axis=0)
    p_meta_k = cat("pmk").reshape(1, 32, N_META, 2, 64)
    p_meta_v = cat("pmv").reshape(1, 32, N_META, 2, 64)
    p_win_k = cat("pwk").reshape(1, 32, 128, 2, 64)
    p_win_v = cat("pwv").reshape(1, 32, 128, 2, 64)
    p_state = cat("pst").reshape(1, 32, 4, 128, 128)
    s_new_k = np.stack([np.asarray(r["snk"]) for r in R], axis=0).reshape(1, 8, DEC_SEQ, 2, 64)
    s_new_v = np.stack([np.asarray(r["snv"]) for r in R], axis=0).reshape(1, 8, DEC_SEQ, 2, 64)
    s_state = np.stack([np.asarray(r["sst"]) for r in R], axis=0).reshape(1, 8, 4, 128, 128)
    return tuple(np.ascontiguousarray(a, dtype=np.float32) for a in (
        y_prompt, y_sample, p_meta_k, p_meta_v, p_win_k, p_win_v, p_state, s_new_k, s_new_v, s_state))
```
